# Optimizing a Trainium2 kernel written in Bass

```python
import jax, jax.numpy as jnp
from jax import lax
import numpy as np

D_MODEL = 2048
BATCH = 2
SEQ = 4096
DEPTH = 4
DEC_BATCH = 32
DEC_SEQ = 8
PAST_LEN = 16384
PAGE_SIZE = 128

WINDOW = 128
A_HEADS = 16
A_KV_HEADS = 4
A_HEAD_DIM = 64
A_GROUP = A_HEADS // A_KV_HEADS
A_WIDTH = A_HEADS * A_HEAD_DIM
A_KV_WIDTH = A_KV_HEADS * A_HEAD_DIM
ROPE_THETA = 10000.0
R_HEADS = 8
R_KEY_DIM = 128
R_VAL_DIM = 128
R_QK_WIDTH = R_HEADS * R_KEY_DIM
R_WIDTH = R_HEADS * R_VAL_DIM
R_CHUNK = 128
D_FF = 5632
CONV_W = 3
EPS = 1e-6
IN_SIZES = (A_WIDTH, A_KV_WIDTH, A_KV_WIDTH, R_QK_WIDTH, R_QK_WIDTH, R_WIDTH, R_WIDTH, D_MODEL, D_MODEL)
N_IN = A_WIDTH + 2 * A_KV_WIDTH + 2 * R_QK_WIDTH + 2 * R_WIDTH + 2 * D_MODEL

kernel_name = "hybrid_swa_sink_retention_convffn_step"

F32 = jnp.float32


def rmsnorm(x, g):
    xf = x.astype(F32)
    y = xf * lax.rsqrt(jnp.mean(xf * xf, axis=-1, keepdims=True) + EPS)
    return (y * g.astype(F32)).astype(x.dtype)


def rope(x, pos):
    dh = x.shape[-1]
    half = dh // 2
    inv = ROPE_THETA ** (-2.0 * jnp.arange(half, dtype=F32) / dh)
    ang = pos[:, None] * inv[None, :]
    c = jnp.cos(ang)[:, None, :]
    s = jnp.sin(ang)[:, None, :]
    xf = x.astype(F32)
    x1, x2 = xf[..., :half], xf[..., half:]
    return jnp.concatenate([x1 * c - x2 * s, x2 * c + x1 * s], axis=-1).astype(x.dtype)


def retnet_rotate(x, pos):
    dk = x.shape[-1]
    half = dk // 2
    inv = 1.0 / (10000.0 ** jnp.linspace(0.0, 1.0, half, dtype=F32))
    ang = pos[:, None] * inv[None, :]
    c = jnp.cos(ang)[:, None, :]
    s = jnp.sin(ang)[:, None, :]
    xf = x.astype(F32)
    xe, xo = xf[..., 0::2], xf[..., 1::2]
    out = jnp.stack([xe * c - xo * s, xo * c + xe * s], axis=-1)
    return out.reshape(x.shape).astype(x.dtype)


def sink_softmax(s, valid, sinks):
    sink = sinks.astype(F32).reshape(A_KV_HEADS, A_GROUP)[:, :, None, None]
    s = jnp.where(valid, s, -jnp.inf)
    m = jnp.maximum(jnp.max(s, axis=-1, keepdims=True), sink)
    p = jnp.exp(s - m)
    denom = jnp.sum(p, axis=-1, keepdims=True) + jnp.exp(sink - m)
    return p / denom


def swa_prompt(q, k, v, sinks):
    B, T = q.shape[0], q.shape[1]
    nb = T // WINDOW
    qb = q.reshape(B, nb, WINDOW, A_KV_HEADS, A_GROUP, A_HEAD_DIM)
    pad = jnp.zeros((B, WINDOW, A_KV_HEADS, A_HEAD_DIM), k.dtype)
    kp = jnp.concatenate([pad, k], axis=1).reshape(B, nb + 1, WINDOW, A_KV_HEADS, A_HEAD_DIM)
    vp = jnp.concatenate([pad, v], axis=1).reshape(B, nb + 1, WINDOW, A_KV_HEADS, A_HEAD_DIM)
    kb = jnp.concatenate([kp[:, :-1], kp[:, 1:]], axis=2)
    vb = jnp.concatenate([vp[:, :-1], vp[:, 1:]], axis=2)
    blk = jnp.arange(nb)[:, None, None]
    qi = jnp.arange(WINDOW)[None, :, None] + WINDOW
    ki = jnp.arange(2 * WINDOW)[None, None, :]
    diff = qi - ki
    valid = (diff >= 0) & (diff < WINDOW) & (blk * WINDOW + ki - WINDOW >= 0)
    s = jnp.einsum('bnqhgd,bnkhd->bnhgqk', qb.astype(F32), kb.astype(F32)) * (A_HEAD_DIM ** -0.5)
    p = sink_softmax(s, valid[None, :, None, None], sinks)
    o = jnp.einsum('bnhgqk,bnkhd->bnqhgd', p, vb.astype(F32))
    return o.reshape(B, T, A_WIDTH).astype(q.dtype)


def swa_sample(q, k_new, v_new, k_cache, v_cache, sinks):
    Bd, L = q.shape[0], q.shape[1]
    wc = k_cache.shape[1]
    kk = jnp.concatenate([k_cache.astype(k_new.dtype), k_new], axis=1)
    vv = jnp.concatenate([v_cache.astype(v_new.dtype), v_new], axis=1)
    qpos = PAST_LEN + jnp.arange(L)
    kpos = jnp.concatenate([PAST_LEN - wc + jnp.arange(wc), PAST_LEN + jnp.arange(L)])
    diff = qpos[:, None] - kpos[None, :]
    valid = (diff >= 0) & (diff < WINDOW)
    qg = q.reshape(Bd, L, A_KV_HEADS, A_GROUP, A_HEAD_DIM)
    s = jnp.einsum('bqhgd,bkhd->bhgqk', qg.astype(F32), kk.astype(F32)) * (A_HEAD_DIM ** -0.5)
    p = sink_softmax(s, valid, sinks)
    o = jnp.einsum('bhgqk,bkhd->bqhgd', p, vv.astype(F32))
    return o.reshape(Bd, L, A_WIDTH).astype(q.dtype)


def retention_chunk(q, k, v, S, log_g):
    L = q.shape[1]
    i = jnp.arange(L, dtype=F32)
    diff = i[:, None] - i[None, :]
    D = jnp.where(diff >= 0, jnp.exp(log_g[:, None, None] * jnp.maximum(diff, 0.0)), 0.0)
    scores = jnp.einsum('bihd,bjhd->bhij', q, k) * D[None]
    o = jnp.einsum('bhij,bjhe->bihe', scores, v)
    q_dec = jnp.exp(log_g[None, :] * (i[:, None] + 1.0))
    o = o + jnp.einsum('bihd,bhde->bihe', q, S) * q_dec[None, :, :, None]
    k_dec = jnp.exp(log_g[None, :] * (L - 1.0 - i)[:, None])
    S_new = jnp.exp(log_g * L)[None, :, None, None] * S + jnp.einsum('bjhd,bjhe->bhde', k * k_dec[None, :, :, None], v)
    return o, S_new


def retention_prompt(q, k, v, log_g):
    B, T, H, dk = q.shape
    dv = v.shape[-1]
    nc = T // R_CHUNK

    def to_chunks(a):
        return a.reshape(B, nc, R_CHUNK, H, a.shape[-1]).swapaxes(0, 1)

    def step(S, qkv):
        qc, kc, vc = qkv
        o, S = retention_chunk(qc, kc, vc, S, log_g)
        return S, o

    S0 = jnp.zeros((B, H, dk, dv), F32)
    S, o = lax.scan(step, S0, (to_chunks(q), to_chunks(k), to_chunks(v)))
    return o.swapaxes(0, 1).reshape(B, T, H, dv), S


def layer(x, pos, log_g, k_win, v_win, S_in, conv_buf,
          g_mix, w_in, sinks, w_proj_a, w_proj_b, w_o, g_ffn, w_up, conv_w, conv_b, w_down):
    B, T, _ = x.shape
    h = rmsnorm(x, g_mix)
    z = h @ w_in
    idx = list(np.cumsum(IN_SIZES)[:-1])
    qa, ka, va, qr, kr, vr, gr, ga, gb = jnp.split(z, idx, axis=-1)

    qa = rope(qa.reshape(B, T, A_HEADS, A_HEAD_DIM), pos)
    ka = rope(ka.reshape(B, T, A_KV_HEADS, A_HEAD_DIM), pos)
    va = va.reshape(B, T, A_KV_HEADS, A_HEAD_DIM)
    if k_win is None:
        oa = swa_prompt(qa, ka, va, sinks)
        new_k, new_v = ka[:, -WINDOW:], va[:, -WINDOW:]
    else:
        oa = swa_sample(qa, ka, va, k_win, v_win, sinks)
        new_k, new_v = ka, va

    qr = retnet_rotate(qr.reshape(B, T, R_HEADS, R_KEY_DIM), pos).astype(F32)
    kr = retnet_rotate(kr.reshape(B, T, R_HEADS, R_KEY_DIM), pos).astype(F32) * (R_KEY_DIM ** -0.5)
    vr = vr.reshape(B, T, R_HEADS, R_VAL_DIM).astype(F32)
    if S_in is None:
        orr, S_new = retention_prompt(qr, kr, vr, log_g)
    else:
        orr, S_new = retention_chunk(qr, kr, vr, S_in.astype(F32), log_g)
    orr = orr * lax.rsqrt(jnp.mean(orr * orr, axis=-1, keepdims=True) + EPS)
    ob = orr.reshape(B, T, R_WIDTH).astype(x.dtype) * jax.nn.silu(gr)

    merged = jax.nn.sigmoid(ga) * (oa @ w_proj_a) + jax.nn.sigmoid(gb) * (ob @ w_proj_b)
    x = x + merged @ w_o

    h2 = rmsnorm(x, g_ffn)
    u, g = jnp.split(h2 @ w_up, [D_FF], axis=-1)
    ubuf = jnp.concatenate([conv_buf.astype(u.dtype), u], axis=1)
    conv = conv_b
    for j in range(CONV_W):
        conv = conv + conv_w[j] * ubuf[:, j:j + T]
    f = jax.nn.gelu(conv) * g
    x = x + f @ w_down
    conv_new = ubuf[:, -(CONV_W - 1):]
    return x, new_k, new_v, S_new, conv_new


def setup_inputs(seed: int = 0) -> dict:
    key = jax.random.key(seed)
    ks = jax.random.split(key, 20)
    nrm = jax.random.normal
    return {
        "x_prompt": nrm(ks[0], (BATCH, SEQ, D_MODEL), F32),
        "x_sample": nrm(ks[1], (DEC_BATCH, DEC_SEQ, D_MODEL), F32),
        "cache_win_k": nrm(ks[2], (DEPTH, DEC_BATCH, WINDOW, A_KV_HEADS, A_HEAD_DIM), F32),
        "cache_win_v": nrm(ks[3], (DEPTH, DEC_BATCH, WINDOW, A_KV_HEADS, A_HEAD_DIM), F32),
        "state_ret": 0.1 * nrm(ks[4], (DEPTH, DEC_BATCH, R_HEADS, R_KEY_DIM, R_VAL_DIM), F32),
        "state_conv": nrm(ks[5], (DEPTH, DEC_BATCH, CONV_W - 1, D_FF), F32),
        "g_mix": 1.0 + 0.01 * nrm(ks[6], (DEPTH, D_MODEL), F32),
        "w_in": nrm(ks[7], (DEPTH, D_MODEL, N_IN), F32) * D_MODEL ** -0.5,
        "sinks": 0.5 * nrm(ks[8], (DEPTH, A_HEADS), F32),
        "w_proj_a": nrm(ks[9], (DEPTH, A_WIDTH, D_MODEL), F32) * A_WIDTH ** -0.5,
        "w_proj_b": nrm(ks[10], (DEPTH, R_WIDTH, D_MODEL), F32) * R_WIDTH ** -0.5,
        "w_o": nrm(ks[11], (DEPTH, D_MODEL, D_MODEL), F32) * D_MODEL ** -0.5,
        "g_ffn": 1.0 + 0.01 * nrm(ks[12], (DEPTH, D_MODEL), F32),
        "w_up": nrm(ks[13], (DEPTH, D_MODEL, 2 * D_FF), F32) * D_MODEL ** -0.5,
        "conv_w": nrm(ks[14], (DEPTH, CONV_W, D_FF), F32) * CONV_W ** -0.5,
        "conv_b": 0.01 * nrm(ks[15], (DEPTH, D_FF), F32),
        "w_down": nrm(ks[16], (DEPTH, D_FF, D_MODEL), F32) * D_FF ** -0.5,
        "g_final": 1.0 + 0.01 * nrm(ks[17], (D_MODEL,), F32),
    }


def reference(x_prompt, x_sample, cache_win_k, cache_win_v, state_ret, state_conv,
              g_mix, w_in, sinks, w_proj_a, w_proj_b, w_o, g_ffn, w_up, conv_w, conv_b, w_down, g_final):
    log_g = jnp.log1p(-jnp.exp2(-5.0 - jnp.arange(R_HEADS, dtype=F32)))
    pos_p = jnp.arange(x_prompt.shape[1], dtype=F32)
    pos_s = PAST_LEN + jnp.arange(x_sample.shape[1], dtype=F32)
    xp, xs = x_prompt, x_sample
    kp_l, vp_l, sp_l, cp_l = [], [], [], []
    ks_l, vs_l, ss_l, cs_l = [], [], [], []
    for l in range(DEPTH):
        w = (g_mix[l], w_in[l], sinks[l], w_proj_a[l], w_proj_b[l], w_o[l],
             g_ffn[l], w_up[l], conv_w[l], conv_b[l], w_down[l])
        conv0 = jnp.zeros((xp.shape[0], CONV_W - 1, D_FF), xp.dtype)
        xp, k1, v1, s1, c1 = layer(xp, pos_p, log_g, None, None, None, conv0, *w)
        xs, k2, v2, s2, c2 = layer(xs, pos_s, log_g, cache_win_k[l], cache_win_v[l],
                                   state_ret[l], state_conv[l], *w)
        kp_l.append(k1); vp_l.append(v1); sp_l.append(s1.astype(xp.dtype)); cp_l.append(c1)
        ks_l.append(k2); vs_l.append(v2); ss_l.append(s2.astype(state_ret.dtype)); cs_l.append(c2)
    y_prompt = rmsnorm(xp, g_final)
    y_sample = rmsnorm(xs, g_final)
    return (y_prompt, y_sample,
            jnp.stack(kp_l), jnp.stack(vp_l), jnp.stack(sp_l), jnp.stack(cp_l),
            jnp.stack(ks_l), jnp.stack(vs_l), jnp.stack(ss_l), jnp.stack(cs_l))
```

```python
import os
import numpy as np
from contextlib import ExitStack
import concourse.bass as bass
import concourse.mybir as mybir
from concourse.bass_utils import run_bass_kernel_spmd

F32 = mybir.dt.float32
BF16 = mybir.dt.bfloat16
AF = mybir.ActivationFunctionType
ALU = mybir.AluOpType

NCORES = 8
D = 2048
KC = 16
DFF = 5632
NFC = 44
NIN = 9728
EPS = 1e-6
PAST = 16384


class _Stop(Exception):
    pass


class Cfg:
    stop = None
    stop_pass = 0

    def __init__(self, depth=4, nblk=2, npass=16):
        self.depth = depth
        self.nblk = nblk
        self.npass = npass
        self.nsq = 1
        self.PT = 128 * nblk
        self.L = self.PT + 8 * self.nsq
        self.TOK = self.L * npass
        self.seg = self.PT
        self.seq = self.PT * npass
        self.nb = nblk + self.nsq

    def blocks(self):
        bl = [("p", 128 * j, 128, j) for j in range(self.nblk)]
        bl += [("s", self.PT + 8 * s, 8, s) for s in range(self.nsq)]
        return bl


class Op:
    __slots__ = ("eng", "emit", "deps", "signal", "cnt", "dma_key", "dma_val", "is_dma", "gi")


class Tracker:
    COMPUTE = ("pe", "act", "dve", "pool")

    def __init__(self, sync_same=False):
        self.ops = {e: [] for e in ("pe", "act", "dve", "pool", "sp")}
        self.last_w = {}
        self.readers = {}
        self.sync_same = sync_same
        self.dma_last = {}
        self.gi = 0
        self.out_dmas = []

    def _deps(self, reads, writes):
        deps = []
        for r in reads:
            w = self.last_w.get(r)
            if w is not None:
                deps.append(w)
        for w_ in writes:
            w = self.last_w.get(w_)
            if w is not None:
                deps.append(w)
            deps.extend(self.readers.get(w_, ()))
        return deps

    def _commit(self, op, reads, writes):
        for r in reads:
            self.readers.setdefault(r, []).append(op)
        for w_ in writes:
            self.last_w[w_] = op
            self.readers[w_] = []

    def op(self, eng, emit, reads=(), writes=(), force_same=False):
        o = Op()
        o.eng = eng
        o.emit = emit
        o.is_dma = False
        o.signal = False
        o.cnt = 0
        o.dma_key = None
        o.dma_val = 0
        o.gi = self.gi
        self.gi += 1
        writes = list(writes) + [r for r in reads if r[0] in ("ps", "psT") and r not in writes]
        deps = self._deps(reads, writes)
        o.deps = [d for d in deps if d.is_dma or d.eng != eng or force_same or (self.sync_same and eng != "pe")]
        self._commit(o, reads, writes)
        self.ops[eng].append(o)
        return o

    def dma(self, queue, key, emit, reads=(), writes=(), is_out=False, inc=16):
        o = Op()
        o.eng = queue
        o.emit = emit
        o.is_dma = True
        o.signal = True
        o.cnt = 0
        o.dma_key = key
        o.gi = self.gi
        self.gi += 1
        deps = self._deps(reads, writes)
        prev = self.dma_last.get(key)
        if prev is not None:
            deps.append(prev)
            o.dma_val = prev.dma_val + inc
        else:
            o.dma_val = inc
        o.cnt = inc
        self.dma_last[key] = o
        o.deps = deps
        self._commit(o, reads, writes)
        self.ops[queue].append(o)
        if is_out:
            self.out_dmas.append(o)
        return o

    def finalize(self):
        for e in self.ops:
            for o in self.ops[e]:
                for d in o.deps:
                    if not d.is_dma:
                        d.signal = True
        for e in self.COMPUTE:
            n = 0
            for o in self.ops[e]:
                if o.is_dma:
                    continue
                if o.signal:
                    n += 1
                    o.cnt = n

    def replay(self, nc, block, sems, dma_sems):
        engs = {"pe": "tensor", "act": "scalar", "dve": "vector", "pool": "gpsimd", "sp": "sync"}
        tr = self

        def run(ename, eng):
            known = {}
            for o in tr.ops[ename]:
                need = {}
                for d in o.deps:
                    if d.is_dma:
                        k = ("dma", d.dma_key)
                        v = d.dma_val
                    else:
                        k = ("eng", d.eng)
                        v = d.cnt
                    if v > need.get(k, 0):
                        need[k] = v
                for k, v in need.items():
                    if known.get(k, 0) >= v:
                        continue
                    known[k] = v
                    sem = dma_sems[k[1]] if k[0] == "dma" else sems[k[1]]
                    eng.wait_ge(sem, v)
                ins = o.emit(eng)
                if o.is_dma:
                    ins.then_inc(dma_sems[o.dma_key], o.cnt)
                elif o.signal:
                    ins.then_inc(sems[o.eng], 1)
            if ename == "sp":
                for k, o in tr.dma_last.items():
                    if known.get(("dma", k), 0) < o.dma_val:
                        eng.wait_ge(dma_sems[k], o.dma_val)

        for ename, attr in engs.items():
            def section(eng, ename=ename):
                run(ename, eng)
            getattr(block, attr)(section)


def _gammas():
    return (1.0 - np.exp2(-5.0 - np.arange(8, dtype=np.float64)))


class TabLayout:
    def __init__(self, cfg):
        self.off = {}
        self.n = 0
        self.cfg = cfg
        nb = cfg.nb
        P = cfg.npass
        self.add("ropec", P * nb * 32)
        self.add("ropes", P * nb * 32)
        self.add("retc", P * nb * 64)
        self.add("rets", P * nb * 64)
        self.add("maskC", 128)
        self.add("maskP", 128)
        self.add("maskPf", 2 * 128)
        self.add("qdec", 8)
        self.add("kinv", 8)
        self.add("khatP", 8)
        self.add("khatS", 8)
        self.add("ktil", cfg.nblk * 8)
        self.add("ident", 128)

    def add(self, name, n):
        self.off[name] = (self.n, n)
        self.n += n

    def sl(self, name):
        o, n = self.off[name]
        return slice(o, o + n)


def build_tabs(cfg, core):
    tl = TabLayout(cfg)
    T = np.zeros((128, tl.n), np.float32)
    seqi, r = core // 4, core % 4
    nb, P = cfg.nb, cfg.npass
    g = _gammas()
    half = 32
    inv_a = (np.float32(10000.0) ** (np.float32(-2.0) * np.arange(half, dtype=np.float32) / np.float32(64))).astype(np.float32)
    inv_r = (np.float32(1.0) / (np.float32(10000.0) ** np.linspace(0.0, 1.0, 64, dtype=np.float32))).astype(np.float32)
    ropec = np.zeros((128, P, nb, 32), np.float32)
    ropes = np.zeros((128, P, nb, 32), np.float32)
    retc = np.zeros((128, P, nb, 64), np.float32)
    rets = np.zeros((128, P, nb, 64), np.float32)
    for p in range(P):
        for (kind, col0, nq, idx) in cfg.blocks():
            b = idx if kind == "p" else cfg.nblk + idx
            if kind == "p":
                pos = (cfg.PT * p + 128 * idx + np.arange(128)).astype(np.float32)
            else:
                pos = np.zeros(128, np.float32)
                pos[:8] = PAST + np.arange(8)
            ang = (pos[:, None] * inv_a[None, :]).astype(np.float32).astype(np.float64)
            ropec[:, p, b] = np.cos(ang)
            ropes[:, p, b] = np.sin(ang)
            ang = (pos[:, None] * inv_r[None, :]).astype(np.float32).astype(np.float64)
            retc[:, p, b] = np.cos(ang)
            rets[:, p, b] = np.sin(ang)
    T[:, tl.sl("ropec")] = ropec.reshape(128, -1)
    T[:, tl.sl("ropes")] = ropes.reshape(128, -1)
    T[:, tl.sl("retc")] = retc.reshape(128, -1)
    T[:, tl.sl("rets")] = rets.reshape(128, -1)
    j = np.arange(128)[:, None]
    i = np.arange(128)[None, :]
    T[:, tl.sl("maskC")] = (j <= i)
    T[:, tl.sl("maskP")] = (j > i)
    mpf = np.zeros((128, 2, 128), np.float32)
    mpf[:, 1] = (j > i)
    T[:, tl.sl("maskPf")] = mpf.reshape(128, -1)
    jj = np.arange(128, dtype=np.float64)[:, None]
    T[:, tl.sl("qdec")] = g[None, :] ** (jj + 1)
    T[:, tl.sl("kinv")] = (128.0 ** -0.5) * g[None, :] ** (-(jj + 1))
    T[:, tl.sl("khatP")] = (128.0 ** -0.5) * g[None, :] ** (127 - jj)
    T[:, tl.sl("khatS")] = (128.0 ** -0.5) * g[None, :] ** np.maximum(7 - jj, 0)
    kt = np.zeros((128, cfg.nblk, 8), np.float64)
    for b in range(cfg.nblk):
        kt[:, b] = g[None, :] ** (128 * (cfg.nblk - 1 - b))
    T[:, tl.sl("ktil")] = kt.reshape(128, -1)
    T[:, tl.sl("ident")] = np.eye(128)
    return T


SM = 16 + 16 + 3 * NFC + NFC + 16


def build_program(cfg):
    nc = bass.Bass("TRN2", target_bir_lowering=False)
    tr = Tracker(sync_same=True)
    tl = TabLayout(cfg)
    DEP, P, L, TOK, nb, NBLK, NSQ = cfg.depth, cfg.npass, cfg.L, cfg.TOK, cfg.nb, cfg.nblk, cfg.nsq
    PT = cfg.PT
    NS = NSQ * P
    gam = _gammas()

    def din(name, shape):
        return nc.dram_tensor(name, list(shape), F32, kind="ExternalInput").ap()

    def dout(name, shape):
        return nc.dram_tensor(name, list(shape), F32, kind="ExternalOutput").ap()

    d_xT = din("xT", [128, KC, TOK])
    d_tabs = din("tabs", [128, tl.n])
    d_win = din("w_in", [DEP, D, NIN])
    lite = cfg.stop is not None and cfg.stop[0] in "NWKGAB"
    d_wpa = din("w_proj_a", [DEP, 1024, D] if not lite else [1, 128, 8])
    d_wpb = din("w_proj_b", [DEP, 1024, D] if not lite else [1, 128, 8])
    d_wo = din("w_o", [DEP, D, D] if not lite else [1, 128, 8])
    d_wup = din("w_up", [DEP, D, 2 * DFF] if not lite else [1, 128, 8])
    d_wdn = din("w_down", [DEP, DFF, D] if not lite else [1, 128, 8])
    d_small = din("small", [128, DEP * SM + 16])
    d_kcT = din("kcT", [DEP, NS, 128, 512])
    d_vc2 = din("vc2", [DEP, NS, 128, 512])
    d_sret = din("sret", [DEP, NS, 128, 1024])
    d_sconv = din("sconv", [DEP, 128, NS, NFC, 2])

    o_yT = dout("yT", [128, KC, TOK])
    o_wkp = dout("wk_p", [DEP, P, 128, 256])
    o_wvp = dout("wv_p", [DEP, P, 128, 256])
    o_retp = dout("ret_p", [DEP, P, 128, 1024])
    o_convp = dout("conv_p", [DEP, P, 128, NFC, 2])
    o_wks = dout("wk_s", [DEP, NS, 8, 256])
    o_wvs = dout("wv_s", [DEP, NS, 8, 256])
    o_rets = dout("ret_s", [DEP, NS, 128, 1024])
    o_convs = dout("conv_s", [DEP, NS, 128, NFC, 2])

    def idram(name, shape):
        return nc.dram_tensor(name, list(shape), F32).ap()

    d_xs = idram("xs_scratch", [128, KC, TOK])
    exh_s = [[idram(f"exhs_{l}_{p}", [128, 512]) for p in range(P)] for l in range(DEP)]
    es = ExitStack()
    with es:
        E = es.enter_context

        def sb(name, shape, dt=F32):
            return E(nc.sbuf_tensor(name, list(shape), dt))

        x = sb("x", [128, KC, L])
        hT = sb("hT", [128, KC, L + 2], BF16)
        NWB = 3
        wbuf = [sb(f"wb{i}", [128, 8192], BF16) for i in range(NWB)]
        oT = sb("oT", [128, KC, L], BF16)
        tabs = sb("tabs_sb", [128, tl.n])
        small = sb("small_sb", [128, DEP * SM + 16])
        maskC = sb("maskC", [128, 128], BF16)
        maskP = sb("maskP", [128, 128], BF16)
        maskPf = sb("maskPf", [128, 2, 128], BF16)
        identb = sb("identb", [128, 128], BF16)
        onesb = sb("onesb", [128, 128], BF16)
        sinkexp = sb("sinkexp", [128, 16])

        HS = 128 * (NBLK + 1) + 8 * NSQ
        szA = 8 * L + 4 * HS + (NBLK + 1 + NSQ) * 512 + NSQ * 512 + NSQ * 512
        szB = 16 * L + 2 * nb * 1024
        szC = 32 * L
        szF = 8 * L
        RB = max(szA, szB, szC, szF)
        R = sb("R", [128, RB], BF16)

        class Carve:
            def __init__(self):
                self.o = 0

            def take(self, shape):
                n = int(np.prod(shape))
                v = R[:, self.o:self.o + n]
                self.o += n
                assert self.o <= RB
                if len(shape) == 1:
                    return v
                names = " ".join(f"a{i}" for i in range(len(shape)))
                kw = {f"a{i}": shape[i] for i in range(1, len(shape))}
                return v.rearrange(f"q ({names}) -> q {names}", **kw)

        cA = Carve()
        qTa = cA.take([8, L]); kTa = cA.take([2, 2, HS]); vtm2 = cA.take([NBLK + 1 + NSQ, 512])
        kcT = cA.take([NSQ, 2, 2, 128]); vc2 = cA.take([NSQ, 512])
        cB = Carve()
        qTr = cB.take([8, L]); kTr = cB.take([8, L]); khat = cB.take([nb, 1024]); vtm = cB.take([nb, 1024])
        cC = Carve()
        mT = cC.take([KC, L]); sga = cC.take([8, L]); sgb = cC.take([8, L])
        cF = Carve()
        fT = cF.take([2, 4, L])

        UW = L + 2 + 2 * NSQ
        ubuf = sb("ubuf", [128, 2, UW])
        cg = sb("cg", [128, 4, UW])
        convo = sb("convo", [128, 1 + NSQ, NFC, 2])
        rstd = sb("rstd", [128, L])
        sqt = [sb(f"sqt{i}", [128, L], BF16) for i in range(2)]
        rot = [sb(f"rot{i}", [128, 512]) for i in range(3)]
        rotb = [sb(f"rotb{i}", [128, 512], BF16) for i in range(2)]
        ebuf = [sb(f"ebuf{i}", [128, 512], BF16) for i in range(3)]
        gtmp = [sb(f"gtmp{i}", [128, L]) for i in range(2)]
        dent = sb("dent", [128, 512])
        Sst = sb("Sst", [128, 1024])
        Sb = sb("Sbf", [128, 1024], BF16)
        halo = sb("halo", [128, 512])
        hprev = sb("hprev", [128, 32])
        osm = [sb(f"osm{i}", [128, 512]) for i in range(2)]

        NPS = 7
        ps = [E(nc.psum_tensor(f"ps{i}", [128, 512], F32)) for i in range(NPS)]
        psT = E(nc.psum_tensor("psT", [128, 1024], BF16))
        sems = {e: E(nc.semaphore(f"s_{e}")) for e in ("pe", "act", "dve", "pool")}
        dkeys = [f"w{i}" for i in range(NWB)] + [f"in{i}" for i in range(4)] + [f"inc{i}" for i in range(2)] + [f"out{i}" for i in range(4)] + ["cc"]
        dma_sems = {k: E(nc.semaphore(f"d_{k}")) for k in dkeys}
        block = E(nc.Block())

        st = {"ps": 0, "in": 0, "inc": 0, "out": 0, "w": 0, "tmp": {}}

        def nps():
            st["ps"] = (st["ps"] + 1) % NPS
            return st["ps"]

        def rr(name, n):
            v = (st["tmp"].get(name, -1) + 1) % n
            st["tmp"][name] = v
            return v

        def ld(out_ap, in_ap, writes, cast=False, reads=()):
            if cast:
                k = f"inc{st['inc']}"
                st["inc"] = (st["inc"] + 1) % 2
                q = "pool"
            else:
                k = f"in{st['in']}"
                st["in"] = (st["in"] + 1) % 4
                q = "sp"
            return tr.dma(q, k, lambda e: e.dma_start(out=out_ap, in_=in_ap), reads=reads, writes=writes)

        def store(out_ap, in_ap, reads, writes=()):
            k = f"out{st['out']}"
            st["out"] = (st["out"] + 1) % 4
            return tr.dma("sp", k, lambda e: e.dma_start(out=out_ap, in_=in_ap), reads=reads, writes=writes, is_out=True)

        def wload(dram_view, shape):
            s = st["w"]
            st["w"] = (s + 1) % NWB
            n = int(np.prod(shape))
            v = wbuf[s][:, 0:n].rearrange("q (a b) -> q a b", a=shape[0])
            nsp = 1
            step = shape[0] // nsp
            for i in range(nsp):
                tr.dma("pool", f"w{s}", lambda e, i=i: e.dma_start(out=v[:, i * step:(i + 1) * step, :], in_=dram_view[:, i * step:(i + 1) * step, :]), writes=[("wb", s)])
            return s, v

        def tab(name):
            return tabs[:, tl.sl(name)]

        def sm(l, name):
            base = l * SM
            o = {"gmix": (0, 16), "gffn": (16, 16), "cw": (32, 3 * NFC), "cb": (32 + 3 * NFC, NFC), "sink": (32 + 4 * NFC, 16)}[name]
            return small[:, base + o[0]: base + o[0] + o[1]]

        gfin = small[:, DEP * SM: DEP * SM + 16]
        RTOK = [("R",)]
        HT_ALL = [("hT", kc) for kc in range(KC)]

        def release(res):
            tr.op("dve", lambda e: e.memset(dent[:, 0:1], 0.0), reads=list(res), writes=[("R",), ("dent",)])

        ld(tabs[:], d_tabs, [("tabs",)])
        ld(small[:], d_small, [("small",)])
        tr.op("dve", lambda e: e.tensor_copy(out=maskC[:], in_=tab("maskC")), reads=[("tabs",)], writes=[("maskC",)])
        tr.op("dve", lambda e: e.tensor_copy(out=maskP[:], in_=tab("maskP")), reads=[("tabs",)], writes=[("maskP",)])
        tr.op("dve", lambda e: e.tensor_copy(out=maskPf[:], in_=tab("maskPf").rearrange("q (a b) -> q a b", a=2)), reads=[("tabs",)], writes=[("maskPf",)])
        tr.op("dve", lambda e: e.tensor_copy(out=identb[:], in_=tab("ident")), reads=[("tabs",)], writes=[("identb",)])
        tr.op("dve", lambda e: e.memset(onesb[:], 1.0), writes=[("onesb",)])

        def rmsnorm(p, gain_ap, final=False):
            c0 = 0
            bk = nps()
            for kc in range(KC):
                ti = rr("sqt", 2)
                t = sqt[ti]
                tr.op("act", lambda e, t=t, kc=kc: e.activation(out=t[:, :], in_=x[:, kc, c0:c0 + L], func=AF.Square),
                      reads=[("x", kc)], writes=[("sqt", ti)])
                tr.op("pe", lambda e, t=t, kc=kc: e.matmul(ps[bk][:, 0:L], lhsT=onesb[:], rhs=t[:, :], start=(kc == 0), stop=(kc == KC - 1)),
                      reads=[("sqt", ti), ("onesb",)], writes=[("ps", bk)])
            tr.op("dve", lambda e: e.tensor_scalar(out=rstd[:, :], in0=ps[bk][:, 0:L], scalar1=1.0 / D, scalar2=EPS, op0=ALU.mult, op1=ALU.add),
                  reads=[("ps", bk)], writes=[("rstd",)])
            tr.op("act", lambda e: e.activation(out=rstd[:, :], in_=rstd[:, :], func=AF.Sqrt), reads=[("rstd",)], writes=[("rstd",)])
            tr.op("dve", lambda e: e.reciprocal(out=rstd[:, :], in_=rstd[:, :]), reads=[("rstd",)], writes=[("rstd",)])
            for kc in range(KC):
                if not final:
                    tr.op("dve", lambda e, kc=kc: e.scalar_tensor_tensor(out=hT[:, kc, 0:L], in0=x[:, kc, c0:c0 + L], scalar=gain_ap[:, kc:kc + 1], in1=rstd[:, :], op0=ALU.mult, op1=ALU.mult),
                          reads=[("x", kc), ("rstd",), ("small",)], writes=[("hT", kc)])
                else:
                    yi = rr("gtmp", 2)
                    y_ = gtmp[yi]
                    tr.op("dve", lambda e, kc=kc, y_=y_: e.scalar_tensor_tensor(out=y_[:, :], in0=x[:, kc, c0:c0 + L], scalar=gain_ap[:, kc:kc + 1], in1=rstd[:, :], op0=ALU.mult, op1=ALU.mult),
                          reads=[("x", kc), ("rstd",), ("small",)], writes=[("gtmp", yi)])
                    store(o_yT[:, kc, p * L:p * L + L], y_[:, :], [("gtmp", yi)])

        def tm_project(wslot, wv, col0, nq):
            bk = nps()

            def emit(e):
                ins = None
                for kc in range(KC):
                    ins = e.matmul(ps[bk][0:nq, :], lhsT=hT[:, kc, col0:col0 + nq], rhs=wv[:, kc, :], start=(kc == 0), stop=(kc == KC - 1))
                return ins
            tr.op("pe", emit, reads=HT_ALL + [("wb", wslot)], writes=[("ps", bk)])
            return bk

        def fm_project(wslot, lhs_fn, nk, rhs_fn, rhs_res, ncol):
            bk = nps()

            def emit(e):
                ins = None
                for kc in range(nk):
                    ins = e.matmul(ps[bk][:, 0:ncol], lhsT=lhs_fn(kc), rhs=rhs_fn(kc), start=(kc == 0), stop=(kc == nk - 1))
                return ins
            tr.op("pe", emit, reads=list(rhs_res) + [("wb", wslot)], writes=[("ps", bk)])
            return bk

        def transpose_to(src_bf, nq, nchunks, dst_fn, src_res, dst_res):
            def emit(e):
                ins = None
                for c in range(nchunks):
                    ins = e.transpose(psT[:, c * 128: c * 128 + nq], src_bf[0:nq, c * 128:(c + 1) * 128], identb[0:nq, 0:nq])
                return ins
            tr.op("pe", emit, reads=list(src_res) + [("identb",)], writes=[("psT",)])
            for c in range(nchunks):
                tr.op("act", lambda e, c=c: e.activation(out=dst_fn(c), in_=psT[:, c * 128: c * 128 + nq], func=AF.Copy),
                      reads=[("psT",)] + RTOK, writes=list(dst_res))

        def tb4(name, p, b, nq, w):
            return tab(name).rearrange("q (a b d) -> q a b d", a=P, b=nb)[0:nq, p, b, :]

        def rope_a(bk, nq, nheads, p, b, ov, ores, c_lo):
            c_ap = tb4("ropec", p, b, nq, 32)
            s_ap = tb4("ropes", p, b, nq, 32)
            X = ps[bk][0:nq, c_lo:c_lo + nheads * 64].rearrange("q (h j d) -> q h j d", h=nheads, j=2)
            A = rot[0][0:nq, 0:nheads * 64].rearrange("q (h j d) -> q h j d", h=nheads, j=2)
            Bm = rot[1][0:nq, 0:nheads * 64].rearrange("q (h j d) -> q h j d", h=nheads, j=2)
            cb = c_ap.unsqueeze(1).unsqueeze(1).broadcast_to([nq, nheads, 2, 32])
            sb1 = s_ap.unsqueeze(1).broadcast_to([nq, nheads, 32])
            tr.op("dve", lambda e: e.tensor_tensor(out=A, in0=X, in1=cb, op=ALU.mult), reads=[("ps", bk), ("tabs",)], writes=[("rot", 0)])
            tr.op("dve", lambda e: e.tensor_tensor(out=Bm[:, :, 0, :], in0=X[:, :, 1, :], in1=sb1, op=ALU.mult), reads=[("ps", bk), ("tabs",)], writes=[("rot", 1)])
            tr.op("dve", lambda e: e.tensor_tensor(out=Bm[:, :, 1, :], in0=X[:, :, 0, :], in1=sb1, op=ALU.mult), reads=[("ps", bk), ("tabs",)], writes=[("rot", 1)])
            tr.op("dve", lambda e: e.tensor_tensor(out=ov[:, :, 0, :], in0=A[:, :, 0, :], in1=Bm[:, :, 0, :], op=ALU.subtract), reads=[("rot", 0), ("rot", 1)], writes=ores)
            tr.op("dve", lambda e: e.tensor_tensor(out=ov[:, :, 1, :], in0=A[:, :, 1, :], in1=Bm[:, :, 1, :], op=ALU.add), reads=[("rot", 0), ("rot", 1)], writes=ores)

        def rot_r(bk, nq, p, b):
            c_ap = tb4("retc", p, b, nq, 64)
            s_ap = tb4("rets", p, b, nq, 64)
            X = ps[bk][0:nq, :].rearrange("q (h d j) -> q h d j", h=4, j=2)
            A = rot[0][0:nq, :].rearrange("q (h d j) -> q h d j", h=4, j=2)
            Bm = rot[1][0:nq, :].rearrange("q (h d j) -> q h d j", h=4, j=2)
            O = rot[2][0:nq, :].rearrange("q (h d j) -> q h d j", h=4, j=2)
            cb = c_ap.unsqueeze(1).unsqueeze(3).broadcast_to([nq, 4, 64, 2])
            sb1 = s_ap.unsqueeze(1).broadcast_to([nq, 4, 64])
            tr.op("dve", lambda e: e.tensor_tensor(out=A, in0=X, in1=cb, op=ALU.mult), reads=[("ps", bk), ("tabs",)], writes=[("rot", 0)])
            tr.op("dve", lambda e: e.tensor_tensor(out=Bm[:, :, :, 0], in0=X[:, :, :, 1], in1=sb1, op=ALU.mult), reads=[("ps", bk), ("tabs",)], writes=[("rot", 1)])
            tr.op("dve", lambda e: e.tensor_tensor(out=Bm[:, :, :, 1], in0=X[:, :, :, 0], in1=sb1, op=ALU.mult), reads=[("ps", bk), ("tabs",)], writes=[("rot", 1)])
            tr.op("dve", lambda e: e.tensor_tensor(out=O[:, :, :, 0], in0=A[:, :, :, 0], in1=Bm[:, :, :, 0], op=ALU.subtract), reads=[("rot", 0), ("rot", 1)], writes=[("rot", 2)])
            tr.op("dve", lambda e: e.tensor_tensor(out=O[:, :, :, 1], in0=A[:, :, :, 1], in1=Bm[:, :, :, 1], op=ALU.add), reads=[("rot", 0), ("rot", 1)], writes=[("rot", 2)])

        def hdec(name, nq, g4, extra=None):
            t = tab(name)
            if extra is not None:
                t = t.rearrange("q (b h) -> q b h", h=8)[:, extra, :]
            return t[0:nq, 4 * g4:4 * g4 + 4].unsqueeze(2).broadcast_to([nq, 4, 128])

        def allgather(send, recv):
            tr.dma("pool", "cc", lambda e: e.collective_compute("AllGather", ALU.bypass, replica_groups=[list(range(NCORES))], ins=[send], outs=[recv]),
                   reads=[("dram", send.tensor.name)], writes=[("dram", recv.tensor.name)], inc=1)

        def kslot(j):
            return 128 * j

        def swap_copy(kb, src_f32, nq, sres, kbi):
            tr.op("dve", lambda e: e.tensor_copy(out=kb[0:nq, 0:256], in_=src_f32), reads=sres, writes=[("rotb", kbi)])
            ksrc = src_f32.rearrange("q (c t d) -> q c t d", c=2, t=2)
            kdst = kb[0:nq, 256:512].rearrange("q (c t d) -> q c t d", c=2, t=2)
            tr.op("dve", lambda e: e.tensor_copy(out=kdst[:, :, 0, :], in_=ksrc[:, :, 1, :]), reads=sres, writes=[("rotb", kbi)])
            tr.op("dve", lambda e: e.tensor_copy(out=kdst[:, :, 1, :], in_=ksrc[:, :, 0, :]), reads=sres, writes=[("rotb", kbi)])

        def swa_block(l, p, kind, col0, nq, idx):
            for h in range(4):
                c = h // 2
                stp, stc, ob, db = nps(), nps(), nps(), nps()
                if kind == "p":
                    pk = lambda v, bq, c=c: kTa[bq:bq + 64, v, c, kslot(idx):kslot(idx) + 128]
                    ck = lambda v, bq, c=c: kTa[bq:bq + 64, v, c, kslot(idx + 1):kslot(idx + 1) + 128]
                    pv = vtm2[:, idx, h * 128:(h + 1) * 128]
                    cv = vtm2[0:nq, idx + 1, h * 128:(h + 1) * 128]
                    mp = maskPf[:, (0 if p == 0 else 1), :] if idx == 0 else maskP[:, :]
                else:
                    pk = lambda v, bq, c=c: kcT[bq:bq + 64, idx, v, c, :]
                    ck = lambda v, bq, c=c: kTa[bq:bq + 64, v, c, kslot(NBLK + 1) + 8 * idx: kslot(NBLK + 1) + 8 * idx + 8]
                    pv = vc2[:, idx, h * 128:(h + 1) * 128]
                    cv = vtm2[0:nq, NBLK + 1 + idx, h * 128:(h + 1) * 128]
                    mp = maskP[:, :]
                kres = [("kTa",), ("kcT",)]
                vres = [("vtm2",), ("vc2",)]

                def emit_sc(e, gs_, h=h, pk=pk, ck=ck, stp=stp, stc=stc):
                    ins = None
                    for g in gs_:
                        bq = 64 * (g % 2)
                        v = (h % 2) if bq == 0 else 1 - (h % 2)
                        qv = qTa[bq:bq + 64, 2 * h + g // 2, col0:col0 + nq]
                        e.matmul(ps[stp][:, g * nq:(g + 1) * nq], lhsT=pk(v, bq), rhs=qv, start=True, stop=True)
                        ins = e.matmul(ps[stc][0:nq, g * nq:(g + 1) * nq], lhsT=ck(v, bq), rhs=qv, start=True, stop=True)
                    return ins
                tr.op("pe", lambda e, emit_sc=emit_sc: emit_sc(e, (0, 2)), reads=kres + [("qTa",)], writes=[("ps", stp), ("ps", stc)])
                tr.op("pe", lambda e, emit_sc=emit_sc: emit_sc(e, (1, 3)), reads=kres + [("qTa",)], writes=[("ps", stp), ("ps", stc)], force_same=True)
                epi = rr("ebuf", 3); ep = ebuf[epi]
                eci = rr("ebuf", 3); ec = ebuf[eci]
                tr.op("act", lambda e, ep=ep, stp=stp: e.activation(out=ep[:, 0:4 * nq], in_=ps[stp][:, 0:4 * nq], func=AF.Exp, scale=0.125), reads=[("ps", stp)], writes=[("ebuf", epi)])
                tr.op("act", lambda e, ec=ec, stc=stc: e.activation(out=ec[0:nq, 0:4 * nq], in_=ps[stc][0:nq, 0:4 * nq], func=AF.Exp, scale=0.125), reads=[("ps", stc)], writes=[("ebuf", eci)])
                ep3 = ep[:, 0:4 * nq].rearrange("k (g q) -> k g q", g=4)
                ec3 = ec[0:nq, 0:4 * nq].rearrange("k (g q) -> k g q", g=4)
                tr.op("dve", lambda e, ep3=ep3, mp=mp: e.tensor_tensor(out=ep3, in0=ep3, in1=mp[:, 0:nq].unsqueeze(1).broadcast_to([128, 4, nq]), op=ALU.mult),
                      reads=[("ebuf", epi), ("maskP",), ("maskPf",)], writes=[("ebuf", epi)])
                tr.op("dve", lambda e, ec3=ec3: e.tensor_tensor(out=ec3, in0=ec3, in1=maskC[0:nq, 0:nq].unsqueeze(1).broadcast_to([nq, 4, nq]), op=ALU.mult),
                      reads=[("ebuf", eci), ("maskC",)], writes=[("ebuf", eci)])

                def emit_pv(e, ep=ep, ec=ec, pv=pv, cv=cv, ob=ob, db=db):
                    e.matmul(ps[ob][:, 0:4 * nq], lhsT=pv, rhs=ep[:, 0:4 * nq], start=True, stop=False)
                    e.matmul(ps[ob][:, 0:4 * nq], lhsT=cv, rhs=ec[0:nq, 0:4 * nq], start=False, stop=True)
                    e.matmul(ps[db][:, 0:4 * nq], lhsT=onesb[:, :], rhs=ep[:, 0:4 * nq], start=True, stop=False)
                    return e.matmul(ps[db][:, 0:4 * nq], lhsT=onesb[0:nq, :], rhs=ec[0:nq, 0:4 * nq], start=False, stop=True)
                tr.op("pe", emit_pv, reads=vres + [("ebuf", epi), ("ebuf", eci), ("onesb",)], writes=[("ps", ob), ("ps", db)])
                d3 = dent[:, 0:4 * nq].rearrange("k (g q) -> k g q", g=4)
                tr.op("dve", lambda e, d3=d3, db=db, h=h: e.tensor_tensor(out=d3, in0=ps[db][:, 0:4 * nq].rearrange("k (g q) -> k g q", g=4),
                                                                     in1=sinkexp[:, 4 * h:4 * h + 4].unsqueeze(2).broadcast_to([128, 4, nq]), op=ALU.add),
                      reads=[("ps", db), ("sinkexp",)], writes=[("dent",)])
                tr.op("dve", lambda e: e.reciprocal(out=dent[:, 0:4 * nq], in_=dent[:, 0:4 * nq]), reads=[("dent",)], writes=[("dent",)])
                o4 = ps[ob][:, 0:4 * nq].rearrange("k (a t q) -> k a t q", a=2, t=2)
                r4 = dent[:, 0:4 * nq].rearrange("k (a t q) -> k a t q", a=2, t=2)
                for t in range(2):
                    tr.op("dve", lambda e, t=t, o4=o4, r4=r4, h=h: e.tensor_tensor(out=oT[64 * t:64 * t + 64, 2 * h:2 * h + 2, col0:col0 + nq], in0=o4[64 * t:64 * t + 64, :, t, :], in1=r4[64 * t:64 * t + 64, :, t, :], op=ALU.mult),
                          reads=[("ps", ob), ("dent",)], writes=[("oT", "a")])

        def ret_block(l, p, kind, col0, nq, idx, b):
            Lc = 128 if kind == "p" else 8
            for g4 in range(2):
                stb, obk, ub, sbk = nps(), nps(), nps(), nps()

                def emit_st(e, g4=g4, stb=stb):
                    ins = None
                    for hh in range(4):
                        h = 4 * g4 + hh
                        ins = e.matmul(ps[stb][0:nq, hh * nq:(hh + 1) * nq], lhsT=kTr[:, h, col0:col0 + nq], rhs=qTr[:, h, col0:col0 + nq], start=True, stop=True)
                    return ins
                tr.op("pe", emit_st, reads=[("kTr",), ("qTr",)], writes=[("ps", stb)])
                pbi = rr("ebuf", 3); pb_ = ebuf[pbi]
                tr.op("dve", lambda e, pb_=pb_, stb=stb: e.tensor_tensor(out=pb_[0:nq, 0:4 * nq].rearrange("k (g q) -> k g q", g=4), in0=ps[stb][0:nq, 0:4 * nq].rearrange("k (g q) -> k g q", g=4),
                                                                  in1=maskC[0:nq, 0:nq].unsqueeze(1).broadcast_to([nq, 4, nq]), op=ALU.mult),
                      reads=[("ps", stb), ("maskC",)], writes=[("ebuf", pbi)])

                def emit_o(e, g4=g4, pb_=pb_, obk=obk, ub=ub):
                    ins = None
                    for hh in range(4):
                        h = 4 * g4 + hh
                        e.matmul(ps[obk][:, hh * nq:(hh + 1) * nq], lhsT=vtm[0:nq, b, h * 128:(h + 1) * 128], rhs=pb_[0:nq, hh * nq:(hh + 1) * nq], start=True, stop=False)
                        e.matmul(ps[obk][:, hh * nq:(hh + 1) * nq], lhsT=Sb[:, h * 128:(h + 1) * 128], rhs=qTr[:, h, col0:col0 + nq], start=False, stop=True)
                    for hh in range(4):
                        h = 4 * g4 + hh
                        ins = e.matmul(ps[ub][:, hh * 128:(hh + 1) * 128], lhsT=khat[0:nq, b, h * 128:(h + 1) * 128], rhs=vtm[0:nq, b, h * 128:(h + 1) * 128], start=True, stop=True)
                    return ins
                tr.op("pe", emit_o, reads=[("vtm",), ("khat",), ("ebuf", pbi), ("Sb", g4), ("qTr",)], writes=[("ps", obk), ("ps", ub)])
                for hh in range(4):
                    h = 4 * g4 + hh
                    tr.op("dve", lambda e, h=h, hh=hh, ub=ub: e.scalar_tensor_tensor(out=Sst[:, h * 128:(h + 1) * 128], in0=Sst[:, h * 128:(h + 1) * 128], scalar=float(gam[h] ** Lc),
                                                                                in1=ps[ub][:, hh * 128:(hh + 1) * 128], op0=ALU.mult, op1=ALU.add),
                          reads=[("ps", ub), ("Sst", g4)], writes=[("Sst", g4)])
                tr.op("act", lambda e, g4=g4: e.activation(out=Sb[:, 512 * g4:512 * g4 + 512], in_=Sst[:, 512 * g4:512 * g4 + 512], func=AF.Copy), reads=[("Sst", g4)], writes=[("Sb", g4)])
                sqi = rr("ebuf", 3); sq = ebuf[sqi]
                tr.op("act", lambda e, sq=sq, obk=obk: e.activation(out=sq[:, 0:4 * nq], in_=ps[obk][:, 0:4 * nq], func=AF.Square), reads=[("ps", obk)], writes=[("ebuf", sqi)])
                tr.op("pe", lambda e, sq=sq, sbk=sbk: e.matmul(ps[sbk][:, 0:4 * nq], lhsT=onesb[:, :], rhs=sq[:, 0:4 * nq], start=True, stop=True), reads=[("ebuf", sqi), ("onesb",)], writes=[("ps", sbk)])
                tr.op("dve", lambda e, sbk=sbk: e.tensor_scalar(out=dent[:, 0:4 * nq], in0=ps[sbk][:, 0:4 * nq], scalar1=1.0 / 128, scalar2=EPS, op0=ALU.mult, op1=ALU.add), reads=[("ps", sbk)], writes=[("dent",)])
                tr.op("act", lambda e: e.activation(out=dent[:, 0:4 * nq], in_=dent[:, 0:4 * nq], func=AF.Sqrt), reads=[("dent",)], writes=[("dent",)])
                tr.op("dve", lambda e: e.reciprocal(out=dent[:, 0:4 * nq], in_=dent[:, 0:4 * nq]), reads=[("dent",)], writes=[("dent",)])
                tr.op("dve", lambda e, obk=obk, g4=g4: e.tensor_tensor(out=oT[:, 8 + 4 * g4:8 + 4 * g4 + 4, col0:col0 + nq], in0=ps[obk][:, 0:4 * nq].rearrange("k (g q) -> k g q", g=4),
                                                                  in1=dent[:, 0:4 * nq].rearrange("k (g q) -> k g q", g=4), op=ALU.mult),
                      reads=[("ps", obk), ("dent",)], writes=[("oT", "b")])

        def in_group(l, gi):
            return d_win[l].rearrange("(kc q) n -> q kc n", q=128)[:, :, 512 * gi:512 * (gi + 1)]

        blocks = cfg.blocks()

        def bidx(kind, idx):
            return idx if kind == "p" else NBLK + idx

        def body():
          for l in range(DEP):
              tr.op("act", lambda e, l=l: e.activation(out=sinkexp[:], in_=sm(l, "sink"), func=AF.Exp), reads=[("small",)], writes=[("sinkexp",)])
              for p in range(P):
                  c0 = 0
                  xsrc = d_xT if l == 0 else d_xs
                  for kc in range(KC):
                      ld(x[:, kc, :], xsrc[:, kc, p * L:(p + 1) * L], [("x", kc)], reads=[("dram", "xs", p)])
                  rmsnorm(p, sm(l, "gmix"))
                  if cfg.stop == "N" and p == cfg.stop_pass:
                      raise _Stop()
                  for s in range(NSQ):
                      sq_ = p * NSQ + s
                      ld(kcT[:, s].rearrange("q v c k -> q (v c k)"), d_kcT[l, sq_], [("kcT",)], cast=True, reads=RTOK)
                      ld(vc2[:, s, :], d_vc2[l, sq_], [("vc2",)], cast=True, reads=RTOK)
                  ws, wv = wload(in_group(l, 2), [KC, 512])
                  if cfg.stop == "W" and p == cfg.stop_pass:
                      raise _Stop()

                  def chk(tag):
                      if cfg.stop == tag and p == cfg.stop_pass:
                          raise _Stop()
                  for (kind, col0, nq, idx) in blocks:
                      b = bidx(kind, idx)
                      bk = tm_project(ws, wv, col0, nq)
                      chk("K1")
                      slot = idx + 1 if kind == "p" else NBLK + 1 + idx
                      need_out = (kind == "s") or (idx == NBLK - 1)
                      vd = vtm2[0:nq, slot, :].rearrange("q (h t d) -> q h t d", h=4, t=2)
                      vs = ps[bk][0:nq, 256:512].rearrange("q (h d) -> q h d", h=4)
                      for t in range(2):
                          tr.op("act", lambda e, t=t, vd=vd, vs=vs: e.activation(out=vd[:, :, t, :], in_=vs, func=AF.Copy), reads=[("ps", bk)] + RTOK, writes=[("vtm2",)])
                      chk("K2")
                      omi = rr("osm", 2); om = osm[omi]
                      if need_out:
                          tr.op("act", lambda e, om=om, bk=bk, nq=nq: e.activation(out=om[0:nq, 256:512], in_=ps[bk][0:nq, 256:512], func=AF.Copy), reads=[("ps", bk)], writes=[("osm", omi)])
                      kf = om[0:nq, 0:256].rearrange("q (h j d) -> q h j d", h=4, j=2)
                      rope_a(bk, nq, 4, p, b, kf, [("osm", omi)], 0)
                      chk("K3")
                      kbi = rr("rotb", 2); kb = rotb[kbi]
                      swap_copy(kb, om[0:nq, 0:256], nq, [("osm", omi)], kbi)
                      chk("K4")
                      kc0 = kslot(slot) if kind == "p" else kslot(NBLK + 1) + 8 * idx
                      transpose_to(kb, nq, 4, lambda c, kc0=kc0, nq=nq: kTa[:, c // 2, c % 2, kc0:kc0 + nq], [("rotb", kbi)], [("kTa",)])
                      chk("K5")
                      if need_out:
                          if kind == "p":
                              store(o_wkp[l, p], om[:, 0:256], [("osm", omi)])
                              store(o_wvp[l, p], om[:, 256:512], [("osm", omi)])
                              store(exh_s[l][p], om[:, 0:512], [("osm", omi)], writes=[("dram", exh_s[l][p].tensor.name)])
                          else:
                              store(o_wks[l, p * NSQ + idx], om[0:8, 0:256], [("osm", omi)])
                              store(o_wvs[l, p * NSQ + idx], om[0:8, 256:512], [("osm", omi)])
                  for ga_ in range(2):
                      ws, wv = wload(in_group(l, ga_), [KC, 512])
                      for (kind, col0, nq, idx) in blocks:
                          b = bidx(kind, idx)
                          bk = tm_project(ws, wv, col0, nq)
                          kbi = rr("rotb", 2); kb = rotb[kbi]
                          qv = kb[0:nq, :].rearrange("q (h j d) -> q h j d", h=8, j=2)
                          rope_a(bk, nq, 8, p, b, qv, [("rotb", kbi)], 0)
                          transpose_to(kb, nq, 4, lambda c, ga_=ga_, col0=col0, nq=nq: qTa[:, 4 * ga_ + c, col0:col0 + nq], [("rotb", kbi)], [("qTa",)])
                  chk("A1")
                  if p == 0:
                      tr.op("dve", lambda e: e.memset(halo[:], 0.0), writes=[("halo",)])
                  else:
                      ld(halo[:], exh_s[l][p - 1], [("halo",)], reads=[("dram", exh_s[l][p - 1].tensor.name)])
                  kbi = rr("rotb", 2); kb = rotb[kbi]
                  swap_copy(kb, halo[:, 0:256], 128, [("halo",)], kbi)
                  transpose_to(kb, 128, 4, lambda c: kTa[:, c // 2, c % 2, 0:128], [("rotb", kbi)], [("kTa",)])
                  vd = vtm2[:, 0, :].rearrange("q (h t d) -> q h t d", h=4, t=2)
                  vs = halo[:, 256:512].rearrange("q (h d) -> q h d", h=4)
                  for t in range(2):
                      tr.op("dve", lambda e, t=t, vd=vd, vs=vs: e.tensor_copy(out=vd[:, :, t, :], in_=vs), reads=[("halo",)] + RTOK, writes=[("vtm2",)])
                  chk("A2")
                  for (kind, col0, nq, idx) in blocks:
                      swa_block(l, p, kind, col0, nq, idx)
                      chk("A3")
                  release([("oT", "a"), ("kTa",), ("vtm2",), ("kcT",), ("vc2",), ("qTa",)])
                  if cfg.stop == "A" and p == cfg.stop_pass:
                      raise _Stop()

                  for g4 in range(2):
                      ws, wv = wload(in_group(l, 7 + g4), [KC, 512])
                      for (kind, col0, nq, idx) in blocks:
                          b = bidx(kind, idx)
                          bk = tm_project(ws, wv, col0, nq)
                          tr.op("act", lambda e, bk=bk, b=b, nq=nq, g4=g4: e.activation(out=vtm[0:nq, b, 512 * g4:512 * g4 + 512], in_=ps[bk][0:nq, :], func=AF.Copy), reads=[("ps", bk)] + RTOK, writes=[("vtm",)])
                  for g4 in range(2):
                      ws, wv = wload(in_group(l, 5 + g4), [KC, 512])
                      for (kind, col0, nq, idx) in blocks:
                          b = bidx(kind, idx)
                          bk = tm_project(ws, wv, col0, nq)
                          rot_r(bk, nq, p, b)
                          r3 = rot[2][0:nq, :].rearrange("q (h d) -> q h d", h=4)
                          tr.op("dve", lambda e, r3=r3, b=b, nq=nq, g4=g4, kind=kind: e.tensor_tensor(out=khat[0:nq, b, 512 * g4:512 * g4 + 512].rearrange("q (h d) -> q h d", h=4), in0=r3,
                                                                                                  in1=hdec("khatP" if kind == "p" else "khatS", nq, g4), op=ALU.mult),
                                reads=[("rot", 2), ("tabs",)] + RTOK, writes=[("khat",)])
                          kbi = rr("rotb", 2); kb = rotb[kbi]
                          tr.op("dve", lambda e, r3=r3, kb=kb, nq=nq, g4=g4: e.tensor_tensor(out=kb[0:nq, :].rearrange("q (h d) -> q h d", h=4), in0=r3, in1=hdec("kinv", nq, g4), op=ALU.mult),
                                reads=[("rot", 2), ("tabs",)], writes=[("rotb", kbi)])
                          transpose_to(kb, nq, 4, lambda c, g4=g4, col0=col0, nq=nq: kTr[:, 4 * g4 + c, col0:col0 + nq], [("rotb", kbi)], [("kTr",)])
                  for g4 in range(2):
                      ws, wv = wload(in_group(l, 3 + g4), [KC, 512])
                      for (kind, col0, nq, idx) in blocks:
                          b = bidx(kind, idx)
                          bk = tm_project(ws, wv, col0, nq)
                          rot_r(bk, nq, p, b)
                          kbi = rr("rotb", 2); kb = rotb[kbi]
                          tr.op("dve", lambda e, kb=kb, nq=nq, g4=g4: e.tensor_tensor(out=kb[0:nq, :].rearrange("q (h d) -> q h d", h=4), in0=rot[2][0:nq, :].rearrange("q (h d) -> q h d", h=4), in1=hdec("qdec", nq, g4), op=ALU.mult),
                                reads=[("rot", 2), ("tabs",)], writes=[("rotb", kbi)])
                          transpose_to(kb, nq, 4, lambda c, g4=g4, col0=col0, nq=nq: qTr[:, 4 * g4 + c, col0:col0 + nq], [("rotb", kbi)], [("qTr",)])
                  SS = [("Sst", 0), ("Sst", 1)]
                  if p == 0:
                      tr.op("dve", lambda e: e.memset(Sst[:], 0.0), reads=SS, writes=SS)
                  else:
                      ld(Sst[:], o_retp[l, p - 1], SS, reads=[("dram", "retp", l, p - 1)])
                  for g4 in range(2):
                      tr.op("act", lambda e, g4=g4: e.activation(out=Sb[:, 512 * g4:512 * g4 + 512], in_=Sst[:, 512 * g4:512 * g4 + 512], func=AF.Copy), reads=[("Sst", g4)], writes=[("Sb", g4)])
                  for (kind, col0, nq, idx) in blocks:
                      b = bidx(kind, idx)
                      if kind == "s":
                          if idx == 0:
                              store(o_retp[l, p], Sst[:], SS, writes=[("dram", "retp", l, p)])
                          ld(Sst[:], d_sret[l, p * NSQ + idx], SS)
                          for g4 in range(2):
                              tr.op("act", lambda e, g4=g4: e.activation(out=Sb[:, 512 * g4:512 * g4 + 512], in_=Sst[:, 512 * g4:512 * g4 + 512], func=AF.Copy), reads=[("Sst", g4)], writes=[("Sb", g4)])
                      ret_block(l, p, kind, col0, nq, idx, b)
                      if kind == "s":
                          store(o_rets[l, p * NSQ + idx], Sst[:], SS)
                  release([("oT", "b"), ("kTr",), ("qTr",), ("khat",), ("vtm",)])
                  if cfg.stop == "B" and p == cfg.stop_pass:
                      raise _Stop()

                  for g4 in range(2):
                      ws, wv = wload(in_group(l, 9 + g4), [KC, 512])
                      for ch in range(4):
                          bk = fm_project(ws, lambda kc, wv=wv, ch=ch: wv[:, kc, ch * 128:(ch + 1) * 128], KC, lambda kc: hT[:, kc, 0:L], HT_ALL, L)
                          ti = rr("gtmp", 2); t = gtmp[ti]
                          tr.op("act", lambda e, t=t, bk=bk: e.activation(out=t[:, :], in_=ps[bk][:, 0:L], func=AF.Silu), reads=[("ps", bk)], writes=[("gtmp", ti)])
                          tr.op("dve", lambda e, t=t, g4=g4, ch=ch: e.tensor_tensor(out=oT[:, 8 + 4 * g4 + ch, :], in0=oT[:, 8 + 4 * g4 + ch, :], in1=t[:, :], op=ALU.mult),
                                reads=[("gtmp", ti), ("oT", "b")], writes=[("oT", "b")])

                  for half in range(2):
                      for (gbase, dst, nm) in ((11, sga, "sga"), (15, sgb, "sgb")):
                          for q2 in range(2):
                              ws, wv = wload(in_group(l, gbase + 2 * half + q2), [KC, 512])
                              for ch in range(4):
                                  bk = fm_project(ws, lambda kc, wv=wv, ch=ch: wv[:, kc, ch * 128:(ch + 1) * 128], KC, lambda kc: hT[:, kc, 0:L], HT_ALL, L)
                                  tr.op("act", lambda e, bk=bk, dst=dst, q2=q2, ch=ch: e.activation(out=dst[:, 4 * q2 + ch, :], in_=ps[bk][:, 0:L], func=AF.Sigmoid),
                                        reads=[("ps", bk)] + RTOK, writes=[(nm,)])
                      for (dw, src_lo, res, first) in ((d_wpa, 0, [("oT", "a")], True), (d_wpb, 8, [("oT", "b")], False)):
                          ws, wv = wload(dw[l].rearrange("(kc q) n -> q kc n", q=128)[:, :, 1024 * half:1024 * (half + 1)], [8, 1024])
                          for ch in range(8):
                              bk = fm_project(ws, lambda kc, wv=wv, ch=ch: wv[:, kc, ch * 128:(ch + 1) * 128], 8, lambda kc, src_lo=src_lo: oT[:, src_lo + kc, :], res, L)
                              if first:
                                  tr.op("dve", lambda e, bk=bk, ch=ch: e.tensor_tensor(out=sga[:, ch, :], in0=ps[bk][:, 0:L], in1=sga[:, ch, :], op=ALU.mult),
                                        reads=[("ps", bk), ("sga",)], writes=[("sga",)])
                              else:
                                  tr.op("dve", lambda e, bk=bk, ch=ch: e.tensor_tensor(out=sgb[:, ch, :], in0=ps[bk][:, 0:L], in1=sgb[:, ch, :], op=ALU.mult),
                                        reads=[("ps", bk), ("sgb",)], writes=[("sgb",)])
                                  tr.op("dve", lambda e, ch=ch, half=half: e.tensor_tensor(out=mT[:, 8 * half + ch, :], in0=sga[:, ch, :], in1=sgb[:, ch, :], op=ALU.add),
                                        reads=[("sga",), ("sgb",)] + RTOK, writes=[("mT",)])
                  for og in range(4):
                      ws, wv = wload(d_wo[l].rearrange("(kc q) n -> q kc n", q=128)[:, :, 512 * og:512 * (og + 1)], [KC, 512])
                      for ch in range(4):
                          oc = 4 * og + ch
                          bk = fm_project(ws, lambda kc, wv=wv, ch=ch: wv[:, kc, ch * 128:(ch + 1) * 128], KC, lambda kc: mT[:, kc, :], [("mT",)], L)
                          tr.op("dve", lambda e, bk=bk, oc=oc, c0=c0: e.tensor_tensor(out=x[:, oc, c0:c0 + L], in0=x[:, oc, c0:c0 + L], in1=ps[bk][:, 0:L], op=ALU.add),
                                reads=[("ps", bk), ("x", oc)], writes=[("x", oc)])
                  release([("mT",), ("sga",), ("sgb",)])
                  if cfg.stop == "C" and p == cfg.stop_pass:
                      raise _Stop()

                  rmsnorm(p, sm(l, "gffn"))
                  if p == 0:
                      tr.op("dve", lambda e: e.memset(hprev[:], 0.0), reads=[("hprev",)], writes=[("hprev",)])
                  tr.op("dve", lambda e: e.tensor_copy(out=hT[:, :, L:L + 2], in_=hprev[:].rearrange("q (k j) -> q k j", j=2)), reads=[("hprev",)], writes=[("hTh",)])
                  tr.op("dve", lambda e: e.tensor_copy(out=hprev[:].rearrange("q (k j) -> q k j", j=2), in_=hT[:, :, PT - 2:PT]), reads=HT_ALL + [("hprev",)], writes=[("hprev",)])
                  cw = sm(l, "cw")
                  cbp = sm(l, "cb")
                  for j in range(NFC // 4):
                      wsu, wvu = wload(d_wup[l].rearrange("(kc q) n -> q kc n", q=128)[:, :, 512 * j:512 * (j + 1)], [KC, 512])
                      wsg, wvg = wload(d_wup[l].rearrange("(kc q) n -> q kc n", q=128)[:, :, DFF + 512 * j:DFF + 512 * (j + 1)], [KC, 512])
                      fi = rr("fT", 2)
                      for ch in range(4):
                          fc = 4 * j + ch
                          ui = rr("ubuf", 2)
                          ub_ = ubuf[:, ui, :]
                          for s in range(NSQ):
                              o_s = PT + 2 + 10 * s
                              ld(ub_[:, o_s:o_s + 2], d_sconv[l, :, p * NSQ + s, fc, :], [("ubuf", ui)])
                          bk = fm_project(wsu, lambda kc, wvu=wvu, ch=ch: wvu[:, kc, ch * 128:(ch + 1) * 128], KC, lambda kc: hT[:, kc, 0:L + 2], HT_ALL + [("hTh",)], L + 2)
                          segs = [(0, PT, 2)] + [(PT + 8 * s, PT + 8 * s + 8, PT + 2 + 10 * s + 2) for s in range(NSQ)] + [(L, L + 2, 0)]
                          for (lo, hi, dd) in segs:
                              tr.op("act", lambda e, bk=bk, lo=lo, hi=hi, dd=dd, ub_=ub_: e.activation(out=ub_[:, dd:dd + (hi - lo)], in_=ps[bk][:, lo:hi], func=AF.Copy),
                                    reads=[("ps", bk)], writes=[("ubuf", ui)])
                          tr.op("act", lambda e, ub_=ub_, fc=fc: e.activation(out=convo[:, 0, fc, :], in_=ub_[:, PT:PT + 2], func=AF.Copy), reads=[("ubuf", ui)], writes=[("convo",)])
                          for s in range(NSQ):
                              o_s = PT + 2 + 10 * s
                              tr.op("act", lambda e, ub_=ub_, fc=fc, s=s, o_s=o_s: e.activation(out=convo[:, 1 + s, fc, :], in_=ub_[:, o_s + 8:o_s + 10], func=AF.Copy), reads=[("ubuf", ui)], writes=[("convo",)])
                          cv = cg[:, ch, 0:UW - 2]
                          cres = [("cg", ch)]
                          tr.op("dve", lambda e, cv=cv, ub_=ub_, fc=fc, cw=cw, cbp=cbp: e.tensor_scalar(out=cv, in0=ub_[:, 0:UW - 2], scalar1=cw[:, fc:fc + 1], scalar2=cbp[:, fc:fc + 1], op0=ALU.mult, op1=ALU.add),
                                reads=[("ubuf", ui), ("small",)], writes=cres)
                          tr.op("dve", lambda e, cv=cv, ub_=ub_, fc=fc, cw=cw, cbp=cbp: e.scalar_tensor_tensor(out=cv, in0=ub_[:, 1:UW - 1], scalar=cw[:, NFC + fc:NFC + fc + 1], in1=cv, op0=ALU.mult, op1=ALU.add),
                                reads=[("ubuf", ui), ("small",)] + cres, writes=cres)
                          tr.op("dve", lambda e, cv=cv, ub_=ub_, fc=fc, cw=cw, cbp=cbp: e.scalar_tensor_tensor(out=cv, in0=ub_[:, 2:UW], scalar=cw[:, 2 * NFC + fc:2 * NFC + fc + 1], in1=cv, op0=ALU.mult, op1=ALU.add),
                                reads=[("ubuf", ui), ("small",)] + cres, writes=cres)
                          tr.op("act", lambda e, cv=cv: e.activation(out=cv, in_=cv, func=AF.Gelu_apprx_tanh), reads=cres, writes=cres)
                      for ch in range(4):
                          bk = fm_project(wsg, lambda kc, wvg=wvg, ch=ch: wvg[:, kc, ch * 128:(ch + 1) * 128], KC, lambda kc: hT[:, kc, 0:L], HT_ALL, L)
                          fsegs = [(0, PT, 0)] + [(PT + 8 * s, PT + 8 * s + 8, PT + 2 + 10 * s) for s in range(NSQ)]
                          for (lo, hi, ci_) in fsegs:
                              tr.op("dve", lambda e, bk=bk, lo=lo, hi=hi, ci_=ci_, ch=ch, fi=fi: e.tensor_tensor(out=fT[:, fi, ch, lo:hi], in0=ps[bk][:, lo:hi], in1=cg[:, ch, ci_:ci_ + (hi - lo)], op=ALU.mult),
                                    reads=[("ps", bk), ("cg", ch)] + RTOK, writes=[("fT", fi)])
                      wsd, wvd = wload(d_wdn[l][512 * j:512 * (j + 1), :].rearrange("(c q) n -> q c n", q=128), [4, 2048])
                      for oc in range(KC):
                          bk = fm_project(wsd, lambda c, wvd=wvd, oc=oc: wvd[:, c, oc * 128:(oc + 1) * 128], 4, lambda c, fi=fi: fT[:, fi, c, :], [("fT", fi)], L)
                          tr.op("dve", lambda e, bk=bk, oc=oc, c0=c0: e.tensor_tensor(out=x[:, oc, c0:c0 + L], in0=x[:, oc, c0:c0 + L], in1=ps[bk][:, 0:L], op=ALU.add),
                                reads=[("ps", bk), ("x", oc)], writes=[("x", oc)])
                  store(o_convp[l, p], convo[:, 0], [("convo",)])
                  store(o_convs[l, p * NSQ + 0], convo[:, 1], [("convo",)])
                  release([("fT", 0), ("fT", 1)])
                  if cfg.stop == "F" and p == cfg.stop_pass:
                      raise _Stop()
                  if l == DEP - 1:
                      rmsnorm(p, gfin, final=True)
                  else:
                      for kc in range(KC):
                          store(d_xs[:, kc, p * L:(p + 1) * L], x[:, kc, :], [("x", kc)], writes=[("dram", "xs", p)])


        try:
            body()
        except _Stop:
            pass
        tr.finalize()
        tr.replay(nc, block, sems, dma_sems)
    return nc


_PROG_CACHE = {}


def _run(cfg, inp, trace=False):
    DEP, P, L, PT, NSQ = cfg.depth, cfg.npass, cfg.L, cfg.PT, cfg.nsq
    NS = NSQ * P
    f = lambda a: np.ascontiguousarray(np.asarray(a, dtype=np.float32))
    xp, xs = f(inp["x_prompt"]), f(inp["x_sample"])
    ck, cvv = f(inp["cache_win_k"]), f(inp["cache_win_v"])
    sr, sc = f(inp["state_ret"]), f(inp["state_conv"])
    W = {k: f(inp[k])[:DEP] for k in ("w_in", "w_proj_a", "w_proj_b", "w_o", "w_up", "w_down")}
    if cfg.stop is not None and cfg.stop[0] in "NWKGAB":
        for k in ("w_proj_a", "w_proj_b", "w_o", "w_up", "w_down"):
            W[k] = np.zeros((1, 128, 8), np.float32)
    small = np.zeros((128, DEP * SM + 16), np.float32)
    for l in range(DEP):
        b = l * SM
        small[:, b:b + 16] = f(inp["g_mix"])[l].reshape(16, 128).T
        small[:, b + 16:b + 32] = f(inp["g_ffn"])[l].reshape(16, 128).T
        small[:, b + 32:b + 32 + 3 * NFC] = f(inp["conv_w"])[l].reshape(3, NFC, 128).transpose(2, 0, 1).reshape(128, 3 * NFC)
        small[:, b + 32 + 3 * NFC:b + 32 + 4 * NFC] = f(inp["conv_b"])[l].reshape(NFC, 128).T
        small[:, b + 32 + 4 * NFC:b + 48 + 4 * NFC] = f(inp["sinks"])[l][None, :]
    small[:, DEP * SM:] = f(inp["g_final"]).reshape(16, 128).T
    in_maps = []
    for c in range(NCORES):
        seqi = c % 2
        xT = np.empty((128, KC, cfg.TOK), np.float32)
        kcT = np.empty((DEP, NS, 128, 2, 2, 128), np.float32)
        vc2 = np.empty((DEP, NS, 128, 4, 2, 64), np.float32)
        sret = np.empty((DEP, NS, 128, 8, 128), np.float32)
        sconv = np.empty((DEP, 128, NS, NFC, 2), np.float32)
        for p in range(P):
            t0 = PT * p
            xT[:, :, p * L:p * L + PT] = xp[seqi, t0:t0 + PT, :].T.reshape(KC, 128, PT).transpose(1, 0, 2)
            for s in range(NSQ):
                sl = p * NSQ + s
                gs = 4 * c + (sl % 4)
                xT[:, :, p * L + PT + 8 * s:p * L + PT + 8 * s + 8] = xs[gs].T.reshape(KC, 128, 8).transpose(1, 0, 2)
                for l in range(DEP):
                    ckT = ck[l, gs].transpose(1, 2, 0)
                    for v in range(2):
                        for cc in range(2):
                            kcT[l, sl, 0:64, v, cc, :] = ckT[2 * cc + v]
                            kcT[l, sl, 64:128, v, cc, :] = ckT[2 * cc + 1 - v]
                    vc2[l, sl] = np.repeat(cvv[l, gs][:, :, None, :], 2, axis=2)
                    sret[l, sl] = sr[l, gs].transpose(1, 0, 2)
                    sconv[l, :, sl] = sc[l, gs].reshape(2, NFC, 128).transpose(2, 1, 0)
        m = {"xT": xT, "tabs": build_tabs(cfg, c), "small": small,
             "kcT": kcT.reshape(DEP, NS, 128, 512), "vc2": vc2.reshape(DEP, NS, 128, 512),
             "sret": sret.reshape(DEP, NS, 128, 1024), "sconv": sconv}
        m.update(W)
        in_maps.append(m)
    key = (cfg.depth, cfg.nblk, cfg.npass)
    if key not in _PROG_CACHE:
        _PROG_CACHE[key] = build_program(cfg)
    nc = _PROG_CACHE[key]
    res = run_bass_kernel_spmd(nc, in_maps, core_ids=list(range(NCORES)), **({"trace": True} if trace else {}))
    R = res.results
    SEQ = cfg.seq
    y_p = np.empty((2, SEQ, D), np.float32)
    NSG = 32
    y_s = np.empty((NSG, 8, D), np.float32)
    wkp = np.empty((DEP, 2, 128, 4, 64), np.float32); wvp = np.empty_like(wkp)
    retp = np.empty((DEP, 2, 8, 128, 128), np.float32)
    convp = np.empty((DEP, 2, 2, DFF), np.float32)
    wks = np.empty((DEP, NSG, 8, 4, 64), np.float32); wvs = np.empty_like(wks)
    rets = np.empty((DEP, NSG, 8, 128, 128), np.float32)
    convs = np.empty((DEP, NSG, 2, DFF), np.float32)
    for c in range(NCORES):
        seqi = c % 2
        o = R[c]
        yT = np.asarray(o["yT"])
        for p in range(P):
            t0 = PT * p
            if c < 2:
                y_p[seqi, t0:t0 + PT] = yT[:, :, p * L:p * L + PT].transpose(2, 1, 0).reshape(PT, D)
            for s in range(NSQ):
                sl = p * NSQ + s
                if sl >= 4:
                    continue
                gs = 4 * c + sl
                y_s[gs] = yT[:, :, p * L + PT + 8 * s:p * L + PT + 8 * s + 8].transpose(2, 1, 0).reshape(8, D)
                wks[:, gs] = np.asarray(o["wk_s"])[:, sl].reshape(DEP, 8, 4, 64)
                wvs[:, gs] = np.asarray(o["wv_s"])[:, sl].reshape(DEP, 8, 4, 64)
                rets[:, gs] = np.asarray(o["ret_s"])[:, sl].reshape(DEP, 128, 8, 128).transpose(0, 2, 1, 3)
                convs[:, gs] = np.asarray(o["conv_s"])[:, sl].transpose(0, 3, 2, 1).reshape(DEP, 2, DFF)
        if c < 2:
            wkp[:, seqi] = np.asarray(o["wk_p"])[:, P - 1].reshape(DEP, 128, 4, 64)
            wvp[:, seqi] = np.asarray(o["wv_p"])[:, P - 1].reshape(DEP, 128, 4, 64)
            retp[:, seqi] = np.asarray(o["ret_p"])[:, P - 1].reshape(DEP, 128, 8, 128).transpose(0, 2, 1, 3)
            convp[:, seqi] = np.asarray(o["conv_p"])[:, P - 1].transpose(0, 3, 2, 1).reshape(DEP, 2, DFF)
    outs = (y_p, y_s, wkp, wvp, retp, convp, wks, wvs, rets, convs)
    if trace:
        return outs, res
    return outs


def kernel(**inputs):
    return _run(Cfg(), inputs)
```

```python
import os
import numpy as np
from contextlib import ExitStack
import concourse.bass as bass
import concourse.mybir as mybir
from concourse.bass_utils import run_bass_kernel_spmd

F32 = mybir.dt.float32
BF16 = mybir.dt.bfloat16
AF = mybir.ActivationFunctionType
ALU = mybir.AluOpType

NCORES = 8
D = 2048
KC = 16
DFF = 5632
NFC = 44
NIN = 9728
EPS = 1e-6
PAST = 16384


class _Stop(Exception):
    pass


class Cfg:
    stop = None
    stop_pass = 0

    def __init__(self, depth=4, nblk=2, npass=16):
        self.depth = depth
        self.nblk = nblk
        self.npass = npass
        self.nsq = 1
        self.PT = 128 * nblk
        self.L = self.PT + 8 * self.nsq
        self.TOK = self.L * npass
        self.seg = self.PT
        self.seq = self.PT * npass
        self.nb = nblk + self.nsq

    def blocks(self):
        bl = [("p", 128 * j, 128, j) for j in range(self.nblk)]
        bl += [("s", self.PT + 8 * s, 8, s) for s in range(self.nsq)]
        return bl


class Op:
    __slots__ = ("eng", "emit", "deps", "signal", "cnt", "dma_key", "dma_val", "is_dma", "gi")


class Tracker:
    COMPUTE = ("pe", "act", "dve", "pool")

    def __init__(self, sync_same=False):
        self.ops = {e: [] for e in ("pe", "act", "dve", "pool", "sp")}
        self.last_w = {}
        self.readers = {}
        self.sync_same = sync_same
        self.dma_last = {}
        self.gi = 0
        self.out_dmas = []

    def _deps(self, reads, writes):
        deps = []
        for r in reads:
            w = self.last_w.get(r)
            if w is not None:
                deps.append(w)
        for w_ in writes:
            w = self.last_w.get(w_)
            if w is not None:
                deps.append(w)
            deps.extend(self.readers.get(w_, ()))
        return deps

    def _commit(self, op, reads, writes):
        for r in reads:
            self.readers.setdefault(r, []).append(op)
        for w_ in writes:
            self.last_w[w_] = op
            self.readers[w_] = []

    def op(self, eng, emit, reads=(), writes=(), force_same=False):
        o = Op()
        o.eng = eng
        o.emit = emit
        o.is_dma = False
        o.signal = False
        o.cnt = 0
        o.dma_key = None
        o.dma_val = 0
        o.gi = self.gi
        self.gi += 1
        writes = list(writes) + [r for r in reads if r[0] in ("ps", "psT") and r not in writes]
        deps = self._deps(reads, writes)
        o.deps = [d for d in deps if d.is_dma or d.eng != eng or force_same or (self.sync_same and eng != "pe")]
        self._commit(o, reads, writes)
        self.ops[eng].append(o)
        return o

    def dma(self, queue, key, emit, reads=(), writes=(), is_out=False, inc=16):
        o = Op()
        o.eng = queue
        o.emit = emit
        o.is_dma = True
        o.signal = True
        o.cnt = 0
        o.dma_key = key
        o.gi = self.gi
        self.gi += 1
        deps = self._deps(reads, writes)
        prev = self.dma_last.get(key)
        if prev is not None:
            deps.append(prev)
            o.dma_val = prev.dma_val + inc
        else:
            o.dma_val = inc
        o.cnt = inc
        self.dma_last[key] = o
        o.deps = deps
        self._commit(o, reads, writes)
        self.ops[queue].append(o)
        if is_out:
            self.out_dmas.append(o)
        return o

    def finalize(self):
        for e in self.ops:
            for o in self.ops[e]:
                for d in o.deps:
                    if not d.is_dma:
                        d.signal = True
        for e in self.COMPUTE:
            n = 0
            for o in self.ops[e]:
                if o.is_dma:
                    continue
                if o.signal:
                    n += 1
                    o.cnt = n

    def replay(self, nc, block, sems, dma_sems):
        engs = {"pe": "tensor", "act": "scalar", "dve": "vector", "pool": "gpsimd", "sp": "sync"}
        tr = self

        def run(ename, eng):
            known = {}
            for o in tr.ops[ename]:
                need = {}
                for d in o.deps:
                    if d.is_dma:
                        k = ("dma", d.dma_key)
                        v = d.dma_val
                    else:
                        k = ("eng", d.eng)
                        v = d.cnt
                    if v > need.get(k, 0):
                        need[k] = v
                for k, v in need.items():
                    if known.get(k, 0) >= v:
                        continue
                    known[k] = v
                    sem = dma_sems[k[1]] if k[0] == "dma" else sems[k[1]]
                    eng.wait_ge(sem, v)
                ins = o.emit(eng)
                if o.is_dma:
                    ins.then_inc(dma_sems[o.dma_key], o.cnt)
                elif o.signal:
                    ins.then_inc(sems[o.eng], 1)
            if ename == "sp":
                for k, o in tr.dma_last.items():
                    if known.get(("dma", k), 0) < o.dma_val:
                        eng.wait_ge(dma_sems[k], o.dma_val)

        for ename, attr in engs.items():
            def section(eng, ename=ename):
                run(ename, eng)
            getattr(block, attr)(section)


def _gammas():
    return (1.0 - np.exp2(-5.0 - np.arange(8, dtype=np.float64)))


class TabLayout:
    def __init__(self, cfg):
        self.off = {}
        self.n = 0
        self.cfg = cfg
        nb = cfg.nb
        P = cfg.npass
        self.add("ropec", P * nb * 32)
        self.add("ropes", P * nb * 32)
        self.add("retc", P * nb * 64)
        self.add("rets", P * nb * 64)
        self.add("maskC", 128)
        self.add("maskP", 128)
        self.add("maskPf", 2 * 128)
        self.add("qdec", 8)
        self.add("kinv", 8)
        self.add("khatP", 8)
        self.add("khatS", 8)
        self.add("ktil", cfg.nblk * 8)
        self.add("ident", 128)

    def add(self, name, n):
        self.off[name] = (self.n, n)
        self.n += n

    def sl(self, name):
        o, n = self.off[name]
        return slice(o, o + n)


def build_tabs(cfg, core):
    tl = TabLayout(cfg)
    T = np.zeros((128, tl.n), np.float32)
    seqi, r = core // 4, core % 4
    nb, P = cfg.nb, cfg.npass
    g = _gammas()
    half = 32
    inv_a = (np.float32(10000.0) ** (np.float32(-2.0) * np.arange(half, dtype=np.float32) / np.float32(64))).astype(np.float32)
    inv_r = (np.float32(1.0) / (np.float32(10000.0) ** np.linspace(0.0, 1.0, 64, dtype=np.float32))).astype(np.float32)
    ropec = np.zeros((128, P, nb, 32), np.float32)
    ropes = np.zeros((128, P, nb, 32), np.float32)
    retc = np.zeros((128, P, nb, 64), np.float32)
    rets = np.zeros((128, P, nb, 64), np.float32)
    for p in range(P):
        for (kind, col0, nq, idx) in cfg.blocks():
            b = idx if kind == "p" else cfg.nblk + idx
            if kind == "p":
                pos = (cfg.PT * p + 128 * idx + np.arange(128)).astype(np.float32)
            else:
                pos = np.zeros(128, np.float32)
                pos[:8] = PAST + np.arange(8)
            ang = (pos[:, None] * inv_a[None, :]).astype(np.float32).astype(np.float64)
            ropec[:, p, b] = np.cos(ang)
            ropes[:, p, b] = np.sin(ang)
            ang = (pos[:, None] * inv_r[None, :]).astype(np.float32).astype(np.float64)
            retc[:, p, b] = np.cos(ang)
            rets[:, p, b] = np.sin(ang)
    T[:, tl.sl("ropec")] = ropec.reshape(128, -1)
    T[:, tl.sl("ropes")] = ropes.reshape(128, -1)
    T[:, tl.sl("retc")] = retc.reshape(128, -1)
    T[:, tl.sl("rets")] = rets.reshape(128, -1)
    j = np.arange(128)[:, None]
    i = np.arange(128)[None, :]
    T[:, tl.sl("maskC")] = (j <= i)
    T[:, tl.sl("maskP")] = (j > i)
    mpf = np.zeros((128, 2, 128), np.float32)
    mpf[:, 1] = (j > i)
    T[:, tl.sl("maskPf")] = mpf.reshape(128, -1)
    jj = np.arange(128, dtype=np.float64)[:, None]
    T[:, tl.sl("qdec")] = g[None, :] ** (jj + 1)
    T[:, tl.sl("kinv")] = (128.0 ** -0.5) * g[None, :] ** (-(jj + 1))
    T[:, tl.sl("khatP")] = (128.0 ** -0.5) * g[None, :] ** (127 - jj)
    T[:, tl.sl("khatS")] = (128.0 ** -0.5) * g[None, :] ** np.maximum(7 - jj, 0)
    kt = np.zeros((128, cfg.nblk, 8), np.float64)
    for b in range(cfg.nblk):
        kt[:, b] = g[None, :] ** (128 * (cfg.nblk - 1 - b))
    T[:, tl.sl("ktil")] = kt.reshape(128, -1)
    T[:, tl.sl("ident")] = np.eye(128)
    return T


SM = 16 + 16 + 3 * NFC + NFC + 16


def build_program(cfg):
    nc = bass.Bass("TRN2", target_bir_lowering=False)
    tr = Tracker(sync_same=True)
    tl = TabLayout(cfg)
    DEP, P, L, TOK, nb, NBLK, NSQ = cfg.depth, cfg.npass, cfg.L, cfg.TOK, cfg.nb, cfg.nblk, cfg.nsq
    PT = cfg.PT
    NS = NSQ * P
    gam = _gammas()

    def din(name, shape):
        return nc.dram_tensor(name, list(shape), F32, kind="ExternalInput").ap()

    def dout(name, shape):
        return nc.dram_tensor(name, list(shape), F32, kind="ExternalOutput").ap()

    d_xT = din("xT", [128, KC, TOK])
    d_tabs = din("tabs", [128, tl.n])
    d_win = din("w_in", [DEP, D, NIN])
    lite = cfg.stop is not None and cfg.stop[0] in "NWKGAB"
    d_wpa = din("w_proj_a", [DEP, 1024, D] if not lite else [1, 128, 8])
    d_wpb = din("w_proj_b", [DEP, 1024, D] if not lite else [1, 128, 8])
    d_wo = din("w_o", [DEP, D, D] if not lite else [1, 128, 8])
    d_wup = din("w_up", [DEP, D, 2 * DFF] if not lite else [1, 128, 8])
    d_wdn = din("w_down", [DEP, DFF, D] if not lite else [1, 128, 8])
    d_small = din("small", [128, DEP * SM + 16])
    d_kcT = din("kcT", [DEP, NS, 128, 512])
    d_vc2 = din("vc2", [DEP, NS, 128, 512])
    d_sret = din("sret", [DEP, NS, 128, 1024])
    d_sconv = din("sconv", [DEP, 128, NS, NFC, 2])

    o_yT = dout("yT", [128, KC, TOK])
    o_wkp = dout("wk_p", [DEP, P, 128, 256])
    o_wvp = dout("wv_p", [DEP, P, 128, 256])
    o_retp = dout("ret_p", [DEP, P, 128, 1024])
    o_convp = dout("conv_p", [DEP, P, 128, NFC, 2])
    o_wks = dout("wk_s", [DEP, NS, 8, 256])
    o_wvs = dout("wv_s", [DEP, NS, 8, 256])
    o_rets = dout("ret_s", [DEP, NS, 128, 1024])
    o_convs = dout("conv_s", [DEP, NS, 128, NFC, 2])

    def idram(name, shape):
        return nc.dram_tensor(name, list(shape), F32).ap()

    d_xs = idram("xs_scratch", [128, KC, TOK])
    exh_s = [[idram(f"exhs_{l}_{p}", [128, 512]) for p in range(P)] for l in range(DEP)]
    es = ExitStack()
    with es:
        E = es.enter_context

        def sb(name, shape, dt=F32):
            return E(nc.sbuf_tensor(name, list(shape), dt))

        x = sb("x", [128, KC, L])
        hT = sb("hT", [128, KC, L + 2], BF16)
        NWB = 4
        wbuf = [sb(f"wb{i}", [128, 8192], BF16) for i in range(NWB)]
        oT = sb("oT", [128, KC, L], BF16)
        tabs = sb("tabs_sb", [128, tl.n])
        small = sb("small_sb", [128, DEP * SM + 16])
        maskC = sb("maskC", [128, 128], BF16)
        maskP = sb("maskP", [128, 128], BF16)
        maskPf = sb("maskPf", [128, 2, 128], BF16)
        identb = sb("identb", [128, 128], BF16)
        onesb = sb("onesb", [128, 128], BF16)
        sinkexp = sb("sinkexp", [128, 16])

        HS = 128 * (NBLK + 1) + 8 * NSQ
        szA = 8 * L + 4 * HS + (NBLK + 1 + NSQ) * 512 + NSQ * 512 + NSQ * 512
        szB = 16 * L + 2 * nb * 1024
        szC = 32 * L
        szF = 8 * L
        RB = max(szA, szB, szC, szF)
        R = sb("R", [128, RB], BF16)

        class Carve:
            def __init__(self):
                self.o = 0

            def take(self, shape):
                n = int(np.prod(shape))
                v = R[:, self.o:self.o + n]
                self.o += n
                assert self.o <= RB
                if len(shape) == 1:
                    return v
                names = " ".join(f"a{i}" for i in range(len(shape)))
                kw = {f"a{i}": shape[i] for i in range(1, len(shape))}
                return v.rearrange(f"q ({names}) -> q {names}", **kw)

        cA = Carve()
        qTa = cA.take([8, L]); kTa = cA.take([2, 2, HS]); vtm2 = cA.take([NBLK + 1 + NSQ, 512])
        kcT = cA.take([NSQ, 2, 2, 128]); vc2 = cA.take([NSQ, 512])
        cB = Carve()
        qTr = cB.take([8, L]); kTr = cB.take([8, L]); khat = cB.take([nb, 1024]); vtm = cB.take([nb, 1024])
        cC = Carve()
        mT = cC.take([KC, L]); sga = cC.take([8, L]); sgb = cC.take([8, L])
        cF = Carve()
        fT = cF.take([2, 4, L])

        UW = L + 2 + 2 * NSQ
        ubuf = sb("ubuf", [128, 2, UW])
        cg = sb("cg", [128, 4, UW])
        convo = sb("convo", [128, 1 + NSQ, NFC, 2])
        rstd = sb("rstd", [128, L])
        sqt = [sb(f"sqt{i}", [128, L], BF16) for i in range(2)]
        rot = [sb(f"rot{i}", [128, 512]) for i in range(3)]
        rotb = [sb(f"rotb{i}", [128, 512], BF16) for i in range(2)]
        ebuf = [sb(f"ebuf{i}", [128, 512], BF16) for i in range(4)]
        gtmp = [sb(f"gtmp{i}", [128, L]) for i in range(2)]
        dent = sb("dent", [128, 512])
        Sst = sb("Sst", [128, 1024])
        Sb = sb("Sbf", [128, 1024], BF16)
        halo = sb("halo", [128, 512])
        hprev = sb("hprev", [128, 32])
        osm = [sb(f"osm{i}", [128, 512]) for i in range(2)]

        NPS = 7
        ps = [E(nc.psum_tensor(f"ps{i}", [128, 512], F32)) for i in range(NPS)]
        psT = E(nc.psum_tensor("psT", [128, 1024], BF16))
        sems = {e: E(nc.semaphore(f"s_{e}")) for e in ("pe", "act", "dve", "pool")}
        dkeys = [f"w{i}" for i in range(NWB)] + [f"in{i}" for i in range(4)] + [f"inc{i}" for i in range(2)] + [f"out{i}" for i in range(4)] + ["cc"]
        dma_sems = {k: E(nc.semaphore(f"d_{k}")) for k in dkeys}
        block = E(nc.Block())

        st = {"ps": 0, "in": 0, "inc": 0, "out": 0, "w": 0, "tmp": {}}

        def nps():
            st["ps"] = (st["ps"] + 1) % NPS
            return st["ps"]

        def rr(name, n):
            v = (st["tmp"].get(name, -1) + 1) % n
            st["tmp"][name] = v
            return v

        def ld(out_ap, in_ap, writes, cast=False, reads=()):
            if cast:
                k = f"inc{st['inc']}"
                st["inc"] = (st["inc"] + 1) % 2
                q = "pool"
            else:
                k = f"in{st['in']}"
                st["in"] = (st["in"] + 1) % 4
                q = "sp"
            return tr.dma(q, k, lambda e: e.dma_start(out=out_ap, in_=in_ap), reads=reads, writes=writes)

        def store(out_ap, in_ap, reads, writes=()):
            k = f"out{st['out']}"
            st["out"] = (st["out"] + 1) % 4
            return tr.dma("sp", k, lambda e: e.dma_start(out=out_ap, in_=in_ap), reads=reads, writes=writes, is_out=True)

        def wload(dram_view, shape):
            s = st["w"]
            st["w"] = (s + 1) % NWB
            n = int(np.prod(shape))
            v = wbuf[s][:, 0:n].rearrange("q (a b) -> q a b", a=shape[0])
            nsp = 1
            step = shape[0] // nsp
            for i in range(nsp):
                tr.dma("pool", f"w{s}", lambda e, i=i: e.dma_start(out=v[:, i * step:(i + 1) * step, :], in_=dram_view[:, i * step:(i + 1) * step, :]), writes=[("wb", s)])
            return s, v

        def tab(name):
            return tabs[:, tl.sl(name)]

        def sm(l, name):
            base = l * SM
            o = {"gmix": (0, 16), "gffn": (16, 16), "cw": (32, 3 * NFC), "cb": (32 + 3 * NFC, NFC), "sink": (32 + 4 * NFC, 16)}[name]
            return small[:, base + o[0]: base + o[0] + o[1]]

        gfin = small[:, DEP * SM: DEP * SM + 16]
        RTOK = [("R",)]
        HT_ALL = [("hT", kc) for kc in range(KC)]

        def release(res):
            tr.op("dve", lambda e: e.memset(dent[:, 0:1], 0.0), reads=list(res), writes=[("R",), ("dent",)])

        ld(tabs[:], d_tabs, [("tabs",)])
        ld(small[:], d_small, [("small",)])
        tr.op("dve", lambda e: e.tensor_copy(out=maskC[:], in_=tab("maskC")), reads=[("tabs",)], writes=[("maskC",)])
        tr.op("dve", lambda e: e.tensor_copy(out=maskP[:], in_=tab("maskP")), reads=[("tabs",)], writes=[("maskP",)])
        tr.op("dve", lambda e: e.tensor_copy(out=maskPf[:], in_=tab("maskPf").rearrange("q (a b) -> q a b", a=2)), reads=[("tabs",)], writes=[("maskPf",)])
        tr.op("dve", lambda e: e.tensor_copy(out=identb[:], in_=tab("ident")), reads=[("tabs",)], writes=[("identb",)])
        tr.op("dve", lambda e: e.memset(onesb[:], 1.0), writes=[("onesb",)])

        def rmsnorm(p, gain_ap, final=False):
            c0 = 0
            bk = nps()
            for kc in range(KC):
                ti = rr("sqt", 2)
                t = sqt[ti]
                tr.op("act", lambda e, t=t, kc=kc: e.activation(out=t[:, :], in_=x[:, kc, c0:c0 + L], func=AF.Square),
                      reads=[("x", kc)], writes=[("sqt", ti)])
                tr.op("pe", lambda e, t=t, kc=kc: e.matmul(ps[bk][:, 0:L], lhsT=onesb[:], rhs=t[:, :], start=(kc == 0), stop=(kc == KC - 1)),
                      reads=[("sqt", ti), ("onesb",)], writes=[("ps", bk)])
            tr.op("dve", lambda e: e.tensor_scalar(out=rstd[:, :], in0=ps[bk][:, 0:L], scalar1=1.0 / D, scalar2=EPS, op0=ALU.mult, op1=ALU.add),
                  reads=[("ps", bk)], writes=[("rstd",)])
            tr.op("act", lambda e: e.activation(out=rstd[:, :], in_=rstd[:, :], func=AF.Sqrt), reads=[("rstd",)], writes=[("rstd",)])
            tr.op("dve", lambda e: e.reciprocal(out=rstd[:, :], in_=rstd[:, :]), reads=[("rstd",)], writes=[("rstd",)])
            for kc in range(KC):
                if not final:
                    tr.op("dve", lambda e, kc=kc: e.scalar_tensor_tensor(out=hT[:, kc, 0:L], in0=x[:, kc, c0:c0 + L], scalar=gain_ap[:, kc:kc + 1], in1=rstd[:, :], op0=ALU.mult, op1=ALU.mult),
                          reads=[("x", kc), ("rstd",), ("small",)], writes=[("hT", kc)])
                else:
                    yi = rr("gtmp", 2)
                    y_ = gtmp[yi]
                    tr.op("dve", lambda e, kc=kc, y_=y_: e.scalar_tensor_tensor(out=y_[:, :], in0=x[:, kc, c0:c0 + L], scalar=gain_ap[:, kc:kc + 1], in1=rstd[:, :], op0=ALU.mult, op1=ALU.mult),
                          reads=[("x", kc), ("rstd",), ("small",)], writes=[("gtmp", yi)])
                    store(o_yT[:, kc, p * L:p * L + L], y_[:, :], [("gtmp", yi)])

        def tm_project(wslot, wv, col0, nq):
            bk = nps()

            def emit(e):
                ins = None
                for kc in range(KC):
                    ins = e.matmul(ps[bk][0:nq, :], lhsT=hT[:, kc, col0:col0 + nq], rhs=wv[:, kc, :], start=(kc == 0), stop=(kc == KC - 1))
                return ins
            tr.op("pe", emit, reads=HT_ALL + [("wb", wslot)], writes=[("ps", bk)])
            return bk

        def fm_project(wslot, lhs_fn, nk, rhs_fn, rhs_res, ncol):
            bk = nps()

            def emit(e):
                ins = None
                for kc in range(nk):
                    ins = e.matmul(ps[bk][:, 0:ncol], lhsT=lhs_fn(kc), rhs=rhs_fn(kc), start=(kc == 0), stop=(kc == nk - 1))
                return ins
            tr.op("pe", emit, reads=list(rhs_res) + [("wb", wslot)], writes=[("ps", bk)])
            return bk

        def transpose_to(src_bf, nq, nchunks, dst_fn, src_res, dst_res):
            def emit(e):
                ins = None
                for c in range(nchunks):
                    ins = e.transpose(psT[:, c * 128: c * 128 + nq], src_bf[0:nq, c * 128:(c + 1) * 128], identb[0:nq, 0:nq])
                return ins
            tr.op("pe", emit, reads=list(src_res) + [("identb",)], writes=[("psT",)])
            for c in range(nchunks):
                tr.op("act", lambda e, c=c: e.activation(out=dst_fn(c), in_=psT[:, c * 128: c * 128 + nq], func=AF.Copy),
                      reads=[("psT",)] + RTOK, writes=list(dst_res))

        def tb4(name, p, b, nq, w):
            return tab(name).rearrange("q (a b d) -> q a b d", a=P, b=nb)[0:nq, p, b, :]

        def rope_a(bk, nq, nheads, p, b, ov, ores, c_lo):
            c_ap = tb4("ropec", p, b, nq, 32)
            s_ap = tb4("ropes", p, b, nq, 32)
            X = ps[bk][0:nq, c_lo:c_lo + nheads * 64].rearrange("q (h j d) -> q h j d", h=nheads, j=2)
            A = rot[0][0:nq, 0:nheads * 64].rearrange("q (h j d) -> q h j d", h=nheads, j=2)
            Bm = rot[1][0:nq, 0:nheads * 64].rearrange("q (h j d) -> q h j d", h=nheads, j=2)
            cb = c_ap.unsqueeze(1).unsqueeze(1).broadcast_to([nq, nheads, 2, 32])
            sb1 = s_ap.unsqueeze(1).broadcast_to([nq, nheads, 32])
            tr.op("dve", lambda e: e.tensor_tensor(out=A, in0=X, in1=cb, op=ALU.mult), reads=[("ps", bk), ("tabs",)], writes=[("rot", 0)])
            tr.op("dve", lambda e: e.tensor_tensor(out=Bm[:, :, 0, :], in0=X[:, :, 1, :], in1=sb1, op=ALU.mult), reads=[("ps", bk), ("tabs",)], writes=[("rot", 1)])
            tr.op("dve", lambda e: e.tensor_tensor(out=Bm[:, :, 1, :], in0=X[:, :, 0, :], in1=sb1, op=ALU.mult), reads=[("ps", bk), ("tabs",)], writes=[("rot", 1)])
            tr.op("dve", lambda e: e.tensor_tensor(out=ov[:, :, 0, :], in0=A[:, :, 0, :], in1=Bm[:, :, 0, :], op=ALU.subtract), reads=[("rot", 0), ("rot", 1)], writes=ores)
            tr.op("dve", lambda e: e.tensor_tensor(out=ov[:, :, 1, :], in0=A[:, :, 1, :], in1=Bm[:, :, 1, :], op=ALU.add), reads=[("rot", 0), ("rot", 1)], writes=ores)

        def rot_r(bk, nq, p, b):
            c_ap = tb4("retc", p, b, nq, 64)
            s_ap = tb4("rets", p, b, nq, 64)
            X = ps[bk][0:nq, :].rearrange("q (h d j) -> q h d j", h=4, j=2)
            A = rot[0][0:nq, :].rearrange("q (h d j) -> q h d j", h=4, j=2)
            Bm = rot[1][0:nq, :].rearrange("q (h d j) -> q h d j", h=4, j=2)
            O = rot[2][0:nq, :].rearrange("q (h d j) -> q h d j", h=4, j=2)
            cb = c_ap.unsqueeze(1).unsqueeze(3).broadcast_to([nq, 4, 64, 2])
            sb1 = s_ap.unsqueeze(1).broadcast_to([nq, 4, 64])
            tr.op("dve", lambda e: e.tensor_tensor(out=A, in0=X, in1=cb, op=ALU.mult), reads=[("ps", bk), ("tabs",)], writes=[("rot", 0)])
            tr.op("dve", lambda e: e.tensor_tensor(out=Bm[:, :, :, 0], in0=X[:, :, :, 1], in1=sb1, op=ALU.mult), reads=[("ps", bk), ("tabs",)], writes=[("rot", 1)])
            tr.op("dve", lambda e: e.tensor_tensor(out=Bm[:, :, :, 1], in0=X[:, :, :, 0], in1=sb1, op=ALU.mult), reads=[("ps", bk), ("tabs",)], writes=[("rot", 1)])
            tr.op("dve", lambda e: e.tensor_tensor(out=O[:, :, :, 0], in0=A[:, :, :, 0], in1=Bm[:, :, :, 0], op=ALU.subtract), reads=[("rot", 0), ("rot", 1)], writes=[("rot", 2)])
            tr.op("dve", lambda e: e.tensor_tensor(out=O[:, :, :, 1], in0=A[:, :, :, 1], in1=Bm[:, :, :, 1], op=ALU.add), reads=[("rot", 0), ("rot", 1)], writes=[("rot", 2)])

        def hdec(name, nq, g4, extra=None):
            t = tab(name)
            if extra is not None:
                t = t.rearrange("q (b h) -> q b h", h=8)[:, extra, :]
            return t[0:nq, 4 * g4:4 * g4 + 4].unsqueeze(2).broadcast_to([nq, 4, 128])

        def allgather(send, recv):
            tr.dma("pool", "cc", lambda e: e.collective_compute("AllGather", ALU.bypass, replica_groups=[list(range(NCORES))], ins=[send], outs=[recv]),
                   reads=[("dram", send.tensor.name)], writes=[("dram", recv.tensor.name)], inc=1)

        def kslot(j):
            return 128 * j

        def swap_copy(kb, src_f32, nq, sres, kbi):
            tr.op("dve", lambda e: e.tensor_copy(out=kb[0:nq, 0:256], in_=src_f32), reads=sres, writes=[("rotb", kbi)])
            ksrc = src_f32.rearrange("q (c t d) -> q c t d", c=2, t=2)
            kdst = kb[0:nq, 256:512].rearrange("q (c t d) -> q c t d", c=2, t=2)
            tr.op("dve", lambda e: e.tensor_copy(out=kdst[:, :, 0, :], in_=ksrc[:, :, 1, :]), reads=sres, writes=[("rotb", kbi)])
            tr.op("dve", lambda e: e.tensor_copy(out=kdst[:, :, 1, :], in_=ksrc[:, :, 0, :]), reads=sres, writes=[("rotb", kbi)])

        def swa_block(l, p, kind, col0, nq, idx):
            for h in range(4):
                c = h // 2
                stp, stc, ob, db = nps(), nps(), nps(), nps()
                if kind == "p":
                    pk = lambda v, bq, c=c: kTa[bq:bq + 64, v, c, kslot(idx):kslot(idx) + 128]
                    ck = lambda v, bq, c=c: kTa[bq:bq + 64, v, c, kslot(idx + 1):kslot(idx + 1) + 128]
                    pv = vtm2[:, idx, h * 128:(h + 1) * 128]
                    cv = vtm2[0:nq, idx + 1, h * 128:(h + 1) * 128]
                    mp = maskPf[:, (0 if p == 0 else 1), :] if idx == 0 else maskP[:, :]
                else:
                    pk = lambda v, bq, c=c: kcT[bq:bq + 64, idx, v, c, :]
                    ck = lambda v, bq, c=c: kTa[bq:bq + 64, v, c, kslot(NBLK + 1) + 8 * idx: kslot(NBLK + 1) + 8 * idx + 8]
                    pv = vc2[:, idx, h * 128:(h + 1) * 128]
                    cv = vtm2[0:nq, NBLK + 1 + idx, h * 128:(h + 1) * 128]
                    mp = maskP[:, :]
                kres = [("kTa",), ("kcT",)]
                vres = [("vtm2",), ("vc2",)]

                def emit_sc(e, gs_, h=h, pk=pk, ck=ck, stp=stp, stc=stc):
                    ins = None
                    for g in gs_:
                        bq = 64 * (g % 2)
                        v = (h % 2) if bq == 0 else 1 - (h % 2)
                        qv = qTa[bq:bq + 64, 2 * h + g // 2, col0:col0 + nq]
                        e.matmul(ps[stp][:, g * nq:(g + 1) * nq], lhsT=pk(v, bq), rhs=qv, start=True, stop=True)
                        ins = e.matmul(ps[stc][0:nq, g * nq:(g + 1) * nq], lhsT=ck(v, bq), rhs=qv, start=True, stop=True)
                    return ins
                tr.op("pe", lambda e, emit_sc=emit_sc: emit_sc(e, (0, 2)), reads=kres + [("qTa",)], writes=[("ps", stp), ("ps", stc)])
                tr.op("pe", lambda e, emit_sc=emit_sc: emit_sc(e, (1, 3)), reads=kres + [("qTa",)], writes=[("ps", stp), ("ps", stc)], force_same=True)
                epi = rr("ebuf", 4); ep = ebuf[epi]
                eci = rr("ebuf", 4); ec = ebuf[eci]
                tr.op("act", lambda e, ep=ep, stp=stp: e.activation(out=ep[:, 0:4 * nq], in_=ps[stp][:, 0:4 * nq], func=AF.Exp, scale=0.125), reads=[("ps", stp)], writes=[("ebuf", epi)])
                tr.op("act", lambda e, ec=ec, stc=stc: e.activation(out=ec[0:nq, 0:4 * nq], in_=ps[stc][0:nq, 0:4 * nq], func=AF.Exp, scale=0.125), reads=[("ps", stc)], writes=[("ebuf", eci)])
                ep3 = ep[:, 0:4 * nq].rearrange("k (g q) -> k g q", g=4)
                ec3 = ec[0:nq, 0:4 * nq].rearrange("k (g q) -> k g q", g=4)
                tr.op("dve", lambda e, ep3=ep3, mp=mp: e.tensor_tensor(out=ep3, in0=ep3, in1=mp[:, 0:nq].unsqueeze(1).broadcast_to([128, 4, nq]), op=ALU.mult),
                      reads=[("ebuf", epi), ("maskP",), ("maskPf",)], writes=[("ebuf", epi)])
                tr.op("dve", lambda e, ec3=ec3: e.tensor_tensor(out=ec3, in0=ec3, in1=maskC[0:nq, 0:nq].unsqueeze(1).broadcast_to([nq, 4, nq]), op=ALU.mult),
                      reads=[("ebuf", eci), ("maskC",)], writes=[("ebuf", eci)])

                def emit_pv(e, ep=ep, ec=ec, pv=pv, cv=cv, ob=ob, db=db):
                    e.matmul(ps[ob][:, 0:4 * nq], lhsT=pv, rhs=ep[:, 0:4 * nq], start=True, stop=False)
                    e.matmul(ps[ob][:, 0:4 * nq], lhsT=cv, rhs=ec[0:nq, 0:4 * nq], start=False, stop=True)
                    e.matmul(ps[db][:, 0:4 * nq], lhsT=onesb[:, :], rhs=ep[:, 0:4 * nq], start=True, stop=False)
                    return e.matmul(ps[db][:, 0:4 * nq], lhsT=onesb[0:nq, :], rhs=ec[0:nq, 0:4 * nq], start=False, stop=True)
                tr.op("pe", emit_pv, reads=vres + [("ebuf", epi), ("ebuf", eci), ("onesb",)], writes=[("ps", ob), ("ps", db)])
                d3 = dent[:, 0:4 * nq].rearrange("k (g q) -> k g q", g=4)
                tr.op("dve", lambda e, d3=d3, db=db, h=h: e.tensor_tensor(out=d3, in0=ps[db][:, 0:4 * nq].rearrange("k (g q) -> k g q", g=4),
                                                                     in1=sinkexp[:, 4 * h:4 * h + 4].unsqueeze(2).broadcast_to([128, 4, nq]), op=ALU.add),
                      reads=[("ps", db), ("sinkexp",)], writes=[("dent",)])
                tr.op("dve", lambda e: e.reciprocal(out=dent[:, 0:4 * nq], in_=dent[:, 0:4 * nq]), reads=[("dent",)], writes=[("dent",)])
                o4 = ps[ob][:, 0:4 * nq].rearrange("k (a t q) -> k a t q", a=2, t=2)
                r4 = dent[:, 0:4 * nq].rearrange("k (a t q) -> k a t q", a=2, t=2)
                for t in range(2):
                    tr.op("dve", lambda e, t=t, o4=o4, r4=r4, h=h: e.tensor_tensor(out=oT[64 * t:64 * t + 64, 2 * h:2 * h + 2, col0:col0 + nq], in0=o4[64 * t:64 * t + 64, :, t, :], in1=r4[64 * t:64 * t + 64, :, t, :], op=ALU.mult),
                          reads=[("ps", ob), ("dent",)], writes=[("oT", "a")])

        def ret_block(l, p, kind, col0, nq, idx, b):
            Lc = 128 if kind == "p" else 8
            for g4 in range(2):
                stb, obk, ub, sbk = nps(), nps(), nps(), nps()

                def emit_st(e, g4=g4, stb=stb):
                    ins = None
                    for hh in range(4):
                        h = 4 * g4 + hh
                        ins = e.matmul(ps[stb][0:nq, hh * nq:(hh + 1) * nq], lhsT=kTr[:, h, col0:col0 + nq], rhs=qTr[:, h, col0:col0 + nq], start=True, stop=True)
                    return ins
                tr.op("pe", emit_st, reads=[("kTr",), ("qTr",)], writes=[("ps", stb)])
                pbi = rr("ebuf", 4); pb_ = ebuf[pbi]
                tr.op("dve", lambda e, pb_=pb_, stb=stb: e.tensor_tensor(out=pb_[0:nq, 0:4 * nq].rearrange("k (g q) -> k g q", g=4), in0=ps[stb][0:nq, 0:4 * nq].rearrange("k (g q) -> k g q", g=4),
                                                                  in1=maskC[0:nq, 0:nq].unsqueeze(1).broadcast_to([nq, 4, nq]), op=ALU.mult),
                      reads=[("ps", stb), ("maskC",)], writes=[("ebuf", pbi)])

                def emit_o(e, g4=g4, pb_=pb_, obk=obk, ub=ub):
                    ins = None
                    for hh in range(4):
                        h = 4 * g4 + hh
                        e.matmul(ps[obk][:, hh * nq:(hh + 1) * nq], lhsT=vtm[0:nq, b, h * 128:(h + 1) * 128], rhs=pb_[0:nq, hh * nq:(hh + 1) * nq], start=True, stop=False)
                        e.matmul(ps[obk][:, hh * nq:(hh + 1) * nq], lhsT=Sb[:, h * 128:(h + 1) * 128], rhs=qTr[:, h, col0:col0 + nq], start=False, stop=True)
                    for hh in range(4):
                        h = 4 * g4 + hh
                        ins = e.matmul(ps[ub][:, hh * 128:(hh + 1) * 128], lhsT=khat[0:nq, b, h * 128:(h + 1) * 128], rhs=vtm[0:nq, b, h * 128:(h + 1) * 128], start=True, stop=True)
                    return ins
                tr.op("pe", emit_o, reads=[("vtm",), ("khat",), ("ebuf", pbi), ("Sb", g4), ("qTr",)], writes=[("ps", obk), ("ps", ub)])
                for hh in range(4):
                    h = 4 * g4 + hh
                    tr.op("dve", lambda e, h=h, hh=hh, ub=ub: e.scalar_tensor_tensor(out=Sst[:, h * 128:(h + 1) * 128], in0=Sst[:, h * 128:(h + 1) * 128], scalar=float(gam[h] ** Lc),
                                                                                in1=ps[ub][:, hh * 128:(hh + 1) * 128], op0=ALU.mult, op1=ALU.add),
                          reads=[("ps", ub), ("Sst", g4)], writes=[("Sst", g4)])
                tr.op("act", lambda e, g4=g4: e.activation(out=Sb[:, 512 * g4:512 * g4 + 512], in_=Sst[:, 512 * g4:512 * g4 + 512], func=AF.Copy), reads=[("Sst", g4)], writes=[("Sb", g4)])
                sqi = rr("ebuf", 4); sq = ebuf[sqi]
                tr.op("act", lambda e, sq=sq, obk=obk: e.activation(out=sq[:, 0:4 * nq], in_=ps[obk][:, 0:4 * nq], func=AF.Square), reads=[("ps", obk)], writes=[("ebuf", sqi)])
                tr.op("pe", lambda e, sq=sq, sbk=sbk: e.matmul(ps[sbk][:, 0:4 * nq], lhsT=onesb[:, :], rhs=sq[:, 0:4 * nq], start=True, stop=True), reads=[("ebuf", sqi), ("onesb",)], writes=[("ps", sbk)])
                tr.op("dve", lambda e, sbk=sbk: e.tensor_scalar(out=dent[:, 0:4 * nq], in0=ps[sbk][:, 0:4 * nq], scalar1=1.0 / 128, scalar2=EPS, op0=ALU.mult, op1=ALU.add), reads=[("ps", sbk)], writes=[("dent",)])
                tr.op("act", lambda e: e.activation(out=dent[:, 0:4 * nq], in_=dent[:, 0:4 * nq], func=AF.Sqrt), reads=[("dent",)], writes=[("dent",)])
                tr.op("dve", lambda e: e.reciprocal(out=dent[:, 0:4 * nq], in_=dent[:, 0:4 * nq]), reads=[("dent",)], writes=[("dent",)])
                tr.op("dve", lambda e, obk=obk, g4=g4: e.tensor_tensor(out=oT[:, 8 + 4 * g4:8 + 4 * g4 + 4, col0:col0 + nq], in0=ps[obk][:, 0:4 * nq].rearrange("k (g q) -> k g q", g=4),
                                                                  in1=dent[:, 0:4 * nq].rearrange("k (g q) -> k g q", g=4), op=ALU.mult),
                      reads=[("ps", obk), ("dent",)], writes=[("oT", "b")])

        def in_group(l, gi):
            return d_win[l].rearrange("(kc q) n -> q kc n", q=128)[:, :, 512 * gi:512 * (gi + 1)]

        blocks = cfg.blocks()

        def bidx(kind, idx):
            return idx if kind == "p" else NBLK + idx

        def body():
          for l in range(DEP):
              tr.op("act", lambda e, l=l: e.activation(out=sinkexp[:], in_=sm(l, "sink"), func=AF.Exp), reads=[("small",)], writes=[("sinkexp",)])
              for p in range(P):
                  c0 = 0
                  xsrc = d_xT if l == 0 else d_xs
                  for kc in range(KC):
                      ld(x[:, kc, :], xsrc[:, kc, p * L:(p + 1) * L], [("x", kc)], reads=[("dram", "xs", p)])
                  rmsnorm(p, sm(l, "gmix"))
                  if cfg.stop == "N" and p == cfg.stop_pass:
                      raise _Stop()
                  for s in range(NSQ):
                      sq_ = p * NSQ + s
                      ld(kcT[:, s].rearrange("q v c k -> q (v c k)"), d_kcT[l, sq_], [("kcT",)], cast=True, reads=RTOK)
                      ld(vc2[:, s, :], d_vc2[l, sq_], [("vc2",)], cast=True, reads=RTOK)
                  ws, wv = wload(in_group(l, 2), [KC, 512])
                  if cfg.stop == "W" and p == cfg.stop_pass:
                      raise _Stop()

                  def chk(tag):
                      if cfg.stop == tag and p == cfg.stop_pass:
                          raise _Stop()
                  for (kind, col0, nq, idx) in blocks:
                      b = bidx(kind, idx)
                      bk = tm_project(ws, wv, col0, nq)
                      chk("K1")
                      slot = idx + 1 if kind == "p" else NBLK + 1 + idx
                      need_out = (kind == "s") or (idx == NBLK - 1)
                      vd = vtm2[0:nq, slot, :].rearrange("q (h t d) -> q h t d", h=4, t=2)
                      vs = ps[bk][0:nq, 256:512].rearrange("q (h d) -> q h d", h=4)
                      for t in range(2):
                          tr.op("act", lambda e, t=t, vd=vd, vs=vs: e.activation(out=vd[:, :, t, :], in_=vs, func=AF.Copy), reads=[("ps", bk)] + RTOK, writes=[("vtm2",)])
                      chk("K2")
                      omi = rr("osm", 2); om = osm[omi]
                      if need_out:
                          tr.op("act", lambda e, om=om, bk=bk, nq=nq: e.activation(out=om[0:nq, 256:512], in_=ps[bk][0:nq, 256:512], func=AF.Copy), reads=[("ps", bk)], writes=[("osm", omi)])
                      kf = om[0:nq, 0:256].rearrange("q (h j d) -> q h j d", h=4, j=2)
                      rope_a(bk, nq, 4, p, b, kf, [("osm", omi)], 0)
                      chk("K3")
                      kbi = rr("rotb", 2); kb = rotb[kbi]
                      swap_copy(kb, om[0:nq, 0:256], nq, [("osm", omi)], kbi)
                      chk("K4")
                      kc0 = kslot(slot) if kind == "p" else kslot(NBLK + 1) + 8 * idx
                      transpose_to(kb, nq, 4, lambda c, kc0=kc0, nq=nq: kTa[:, c // 2, c % 2, kc0:kc0 + nq], [("rotb", kbi)], [("kTa",)])
                      chk("K5")
                      if need_out:
                          if kind == "p":
                              store(o_wkp[l, p], om[:, 0:256], [("osm", omi)])
                              store(o_wvp[l, p], om[:, 256:512], [("osm", omi)])
                              store(exh_s[l][p], om[:, 0:512], [("osm", omi)], writes=[("dram", exh_s[l][p].tensor.name)])
                          else:
                              store(o_wks[l, p * NSQ + idx], om[0:8, 0:256], [("osm", omi)])
                              store(o_wvs[l, p * NSQ + idx], om[0:8, 256:512], [("osm", omi)])
                  for ga_ in range(2):
                      ws, wv = wload(in_group(l, ga_), [KC, 512])
                      for (kind, col0, nq, idx) in blocks:
                          b = bidx(kind, idx)
                          bk = tm_project(ws, wv, col0, nq)
                          kbi = rr("rotb", 2); kb = rotb[kbi]
                          qv = kb[0:nq, :].rearrange("q (h j d) -> q h j d", h=8, j=2)
                          rope_a(bk, nq, 8, p, b, qv, [("rotb", kbi)], 0)
                          transpose_to(kb, nq, 4, lambda c, ga_=ga_, col0=col0, nq=nq: qTa[:, 4 * ga_ + c, col0:col0 + nq], [("rotb", kbi)], [("qTa",)])
                  chk("A1")
                  if p == 0:
                      tr.op("dve", lambda e: e.memset(halo[:], 0.0), writes=[("halo",)])
                  else:
                      ld(halo[:], exh_s[l][p - 1], [("halo",)], reads=[("dram", exh_s[l][p - 1].tensor.name)])
                  kbi = rr("rotb", 2); kb = rotb[kbi]
                  swap_copy(kb, halo[:, 0:256], 128, [("halo",)], kbi)
                  transpose_to(kb, 128, 4, lambda c: kTa[:, c // 2, c % 2, 0:128], [("rotb", kbi)], [("kTa",)])
                  vd = vtm2[:, 0, :].rearrange("q (h t d) -> q h t d", h=4, t=2)
                  vs = halo[:, 256:512].rearrange("q (h d) -> q h d", h=4)
                  for t in range(2):
                      tr.op("dve", lambda e, t=t, vd=vd, vs=vs: e.tensor_copy(out=vd[:, :, t, :], in_=vs), reads=[("halo",)] + RTOK, writes=[("vtm2",)])
                  chk("A2")
                  for (kind, col0, nq, idx) in blocks:
                      swa_block(l, p, kind, col0, nq, idx)
                      chk("A3")
                  release([("oT", "a"), ("kTa",), ("vtm2",), ("kcT",), ("vc2",), ("qTa",)])
                  if cfg.stop == "A" and p == cfg.stop_pass:
                      raise _Stop()

                  for g4 in range(2):
                      ws, wv = wload(in_group(l, 7 + g4), [KC, 512])
                      for (kind, col0, nq, idx) in blocks:
                          b = bidx(kind, idx)
                          bk = tm_project(ws, wv, col0, nq)
                          tr.op("act", lambda e, bk=bk, b=b, nq=nq, g4=g4: e.activation(out=vtm[0:nq, b, 512 * g4:512 * g4 + 512], in_=ps[bk][0:nq, :], func=AF.Copy), reads=[("ps", bk)] + RTOK, writes=[("vtm",)])
                  for g4 in range(2):
                      ws, wv = wload(in_group(l, 5 + g4), [KC, 512])
                      for (kind, col0, nq, idx) in blocks:
                          b = bidx(kind, idx)
                          bk = tm_project(ws, wv, col0, nq)
                          rot_r(bk, nq, p, b)
                          r3 = rot[2][0:nq, :].rearrange("q (h d) -> q h d", h=4)
                          tr.op("dve", lambda e, r3=r3, b=b, nq=nq, g4=g4, kind=kind: e.tensor_tensor(out=khat[0:nq, b, 512 * g4:512 * g4 + 512].rearrange("q (h d) -> q h d", h=4), in0=r3,
                                                                                                  in1=hdec("khatP" if kind == "p" else "khatS", nq, g4), op=ALU.mult),
                                reads=[("rot", 2), ("tabs",)] + RTOK, writes=[("khat",)])
                          kbi = rr("rotb", 2); kb = rotb[kbi]
                          tr.op("dve", lambda e, r3=r3, kb=kb, nq=nq, g4=g4: e.tensor_tensor(out=kb[0:nq, :].rearrange("q (h d) -> q h d", h=4), in0=r3, in1=hdec("kinv", nq, g4), op=ALU.mult),
                                reads=[("rot", 2), ("tabs",)], writes=[("rotb", kbi)])
                          transpose_to(kb, nq, 4, lambda c, g4=g4, col0=col0, nq=nq: kTr[:, 4 * g4 + c, col0:col0 + nq], [("rotb", kbi)], [("kTr",)])
                  for g4 in range(2):
                      ws, wv = wload(in_group(l, 3 + g4), [KC, 512])
                      for (kind, col0, nq, idx) in blocks:
                          b = bidx(kind, idx)
                          bk = tm_project(ws, wv, col0, nq)
                          rot_r(bk, nq, p, b)
                          kbi = rr("rotb", 2); kb = rotb[kbi]
                          tr.op("dve", lambda e, kb=kb, nq=nq, g4=g4: e.tensor_tensor(out=kb[0:nq, :].rearrange("q (h d) -> q h d", h=4), in0=rot[2][0:nq, :].rearrange("q (h d) -> q h d", h=4), in1=hdec("qdec", nq, g4), op=ALU.mult),
                                reads=[("rot", 2), ("tabs",)], writes=[("rotb", kbi)])
                          transpose_to(kb, nq, 4, lambda c, g4=g4, col0=col0, nq=nq: qTr[:, 4 * g4 + c, col0:col0 + nq], [("rotb", kbi)], [("qTr",)])
                  SS = [("Sst", 0), ("Sst", 1)]
                  if p == 0:
                      tr.op("dve", lambda e: e.memset(Sst[:], 0.0), reads=SS, writes=SS)
                  else:
                      ld(Sst[:], o_retp[l, p - 1], SS, reads=[("dram", "retp", l, p - 1)])
                  for g4 in range(2):
                      tr.op("act", lambda e, g4=g4: e.activation(out=Sb[:, 512 * g4:512 * g4 + 512], in_=Sst[:, 512 * g4:512 * g4 + 512], func=AF.Copy), reads=[("Sst", g4)], writes=[("Sb", g4)])
                  for (kind, col0, nq, idx) in blocks:
                      b = bidx(kind, idx)
                      if kind == "s":
                          if idx == 0:
                              store(o_retp[l, p], Sst[:], SS, writes=[("dram", "retp", l, p)])
                          ld(Sst[:], d_sret[l, p * NSQ + idx], SS)
                          for g4 in range(2):
                              tr.op("act", lambda e, g4=g4: e.activation(out=Sb[:, 512 * g4:512 * g4 + 512], in_=Sst[:, 512 * g4:512 * g4 + 512], func=AF.Copy), reads=[("Sst", g4)], writes=[("Sb", g4)])
                      ret_block(l, p, kind, col0, nq, idx, b)
                      if kind == "s":
                          store(o_rets[l, p * NSQ + idx], Sst[:], SS)
                  release([("oT", "b"), ("kTr",), ("qTr",), ("khat",), ("vtm",)])
                  if cfg.stop == "B" and p == cfg.stop_pass:
                      raise _Stop()

                  for g4 in range(2):
                      ws, wv = wload(in_group(l, 9 + g4), [KC, 512])
                      for ch in range(4):
                          bk = fm_project(ws, lambda kc, wv=wv, ch=ch: wv[:, kc, ch * 128:(ch + 1) * 128], KC, lambda kc: hT[:, kc, 0:L], HT_ALL, L)
                          ti = rr("gtmp", 2); t = gtmp[ti]
                          tr.op("act", lambda e, t=t, bk=bk: e.activation(out=t[:, :], in_=ps[bk][:, 0:L], func=AF.Silu), reads=[("ps", bk)], writes=[("gtmp", ti)])
                          tr.op("dve", lambda e, t=t, g4=g4, ch=ch: e.tensor_tensor(out=oT[:, 8 + 4 * g4 + ch, :], in0=oT[:, 8 + 4 * g4 + ch, :], in1=t[:, :], op=ALU.mult),
                                reads=[("gtmp", ti), ("oT", "b")], writes=[("oT", "b")])

                  for half in range(2):
                      for (gbase, dst, nm) in ((11, sga, "sga"), (15, sgb, "sgb")):
                          for q2 in range(2):
                              ws, wv = wload(in_group(l, gbase + 2 * half + q2), [KC, 512])
                              for ch in range(4):
                                  bk = fm_project(ws, lambda kc, wv=wv, ch=ch: wv[:, kc, ch * 128:(ch + 1) * 128], KC, lambda kc: hT[:, kc, 0:L], HT_ALL, L)
                                  tr.op("act", lambda e, bk=bk, dst=dst, q2=q2, ch=ch: e.activation(out=dst[:, 4 * q2 + ch, :], in_=ps[bk][:, 0:L], func=AF.Sigmoid),
                                        reads=[("ps", bk)] + RTOK, writes=[(nm,)])
                      for (dw, src_lo, res, first) in ((d_wpa, 0, [("oT", "a")], True), (d_wpb, 8, [("oT", "b")], False)):
                          ws, wv = wload(dw[l].rearrange("(kc q) n -> q kc n", q=128)[:, :, 1024 * half:1024 * (half + 1)], [8, 1024])
                          for ch in range(8):
                              bk = fm_project(ws, lambda kc, wv=wv, ch=ch: wv[:, kc, ch * 128:(ch + 1) * 128], 8, lambda kc, src_lo=src_lo: oT[:, src_lo + kc, :], res, L)
                              if first:
                                  tr.op("dve", lambda e, bk=bk, ch=ch: e.tensor_tensor(out=sga[:, ch, :], in0=ps[bk][:, 0:L], in1=sga[:, ch, :], op=ALU.mult),
                                        reads=[("ps", bk), ("sga",)], writes=[("sga",)])
                              else:
                                  tr.op("dve", lambda e, bk=bk, ch=ch: e.tensor_tensor(out=sgb[:, ch, :], in0=ps[bk][:, 0:L], in1=sgb[:, ch, :], op=ALU.mult),
                                        reads=[("ps", bk), ("sgb",)], writes=[("sgb",)])
                                  tr.op("dve", lambda e, ch=ch, half=half: e.tensor_tensor(out=mT[:, 8 * half + ch, :], in0=sga[:, ch, :], in1=sgb[:, ch, :], op=ALU.add),
                                        reads=[("sga",), ("sgb",)] + RTOK, writes=[("mT",)])
                  for og in range(4):
                      ws, wv = wload(d_wo[l].rearrange("(kc q) n -> q kc n", q=128)[:, :, 512 * og:512 * (og + 1)], [KC, 512])
                      for ch in range(4):
                          oc = 4 * og + ch
                          bk = fm_project(ws, lambda kc, wv=wv, ch=ch: wv[:, kc, ch * 128:(ch + 1) * 128], KC, lambda kc: mT[:, kc, :], [("mT",)], L)
                          tr.op("dve", lambda e, bk=bk, oc=oc, c0=c0: e.tensor_tensor(out=x[:, oc, c0:c0 + L], in0=x[:, oc, c0:c0 + L], in1=ps[bk][:, 0:L], op=ALU.add),
                                reads=[("ps", bk), ("x", oc)], writes=[("x", oc)])
                  release([("mT",), ("sga",), ("sgb",)])
                  if cfg.stop == "C" and p == cfg.stop_pass:
                      raise _Stop()

                  rmsnorm(p, sm(l, "gffn"))
                  if p == 0:
                      tr.op("dve", lambda e: e.memset(hprev[:], 0.0), reads=[("hprev",)], writes=[("hprev",)])
                  tr.op("dve", lambda e: e.tensor_copy(out=hT[:, :, L:L + 2], in_=hprev[:].rearrange("q (k j) -> q k j", j=2)), reads=[("hprev",)], writes=[("hTh",)])
                  tr.op("dve", lambda e: e.tensor_copy(out=hprev[:].rearrange("q (k j) -> q k j", j=2), in_=hT[:, :, PT - 2:PT]), reads=HT_ALL + [("hprev",)], writes=[("hprev",)])
                  cw = sm(l, "cw")
                  cbp = sm(l, "cb")
                  for j in range(NFC // 4):
                      wsu, wvu = wload(d_wup[l].rearrange("(kc q) n -> q kc n", q=128)[:, :, 512 * j:512 * (j + 1)], [KC, 512])
                      wsg, wvg = wload(d_wup[l].rearrange("(kc q) n -> q kc n", q=128)[:, :, DFF + 512 * j:DFF + 512 * (j + 1)], [KC, 512])
                      fi = rr("fT", 2)
                      for ch in range(4):
                          fc = 4 * j + ch
                          ui = rr("ubuf", 2)
                          ub_ = ubuf[:, ui, :]
                          for s in range(NSQ):
                              o_s = PT + 2 + 10 * s
                              ld(ub_[:, o_s:o_s + 2], d_sconv[l, :, p * NSQ + s, fc, :], [("ubuf", ui)])
                          bk = fm_project(wsu, lambda kc, wvu=wvu, ch=ch: wvu[:, kc, ch * 128:(ch + 1) * 128], KC, lambda kc: hT[:, kc, 0:L + 2], HT_ALL + [("hTh",)], L + 2)
                          segs = [(0, PT, 2)] + [(PT + 8 * s, PT + 8 * s + 8, PT + 2 + 10 * s + 2) for s in range(NSQ)] + [(L, L + 2, 0)]
                          for (lo, hi, dd) in segs:
                              tr.op("act", lambda e, bk=bk, lo=lo, hi=hi, dd=dd, ub_=ub_: e.activation(out=ub_[:, dd:dd + (hi - lo)], in_=ps[bk][:, lo:hi], func=AF.Copy),
                                    reads=[("ps", bk)], writes=[("ubuf", ui)])
                          tr.op("act", lambda e, ub_=ub_, fc=fc: e.activation(out=convo[:, 0, fc, :], in_=ub_[:, PT:PT + 2], func=AF.Copy), reads=[("ubuf", ui)], writes=[("convo",)])
                          for s in range(NSQ):
                              o_s = PT + 2 + 10 * s
                              tr.op("act", lambda e, ub_=ub_, fc=fc, s=s, o_s=o_s: e.activation(out=convo[:, 1 + s, fc, :], in_=ub_[:, o_s + 8:o_s + 10], func=AF.Copy), reads=[("ubuf", ui)], writes=[("convo",)])
                          cv = cg[:, ch, 0:UW - 2]
                          cres = [("cg", ch)]
                          tr.op("dve", lambda e, cv=cv, ub_=ub_, fc=fc, cw=cw, cbp=cbp: e.tensor_scalar(out=cv, in0=ub_[:, 0:UW - 2], scalar1=cw[:, fc:fc + 1], scalar2=cbp[:, fc:fc + 1], op0=ALU.mult, op1=ALU.add),
                                reads=[("ubuf", ui), ("small",)], writes=cres)
                          tr.op("dve", lambda e, cv=cv, ub_=ub_, fc=fc, cw=cw, cbp=cbp: e.scalar_tensor_tensor(out=cv, in0=ub_[:, 1:UW - 1], scalar=cw[:, NFC + fc:NFC + fc + 1], in1=cv, op0=ALU.mult, op1=ALU.add),
                                reads=[("ubuf", ui), ("small",)] + cres, writes=cres)
                          tr.op("dve", lambda e, cv=cv, ub_=ub_, fc=fc, cw=cw, cbp=cbp: e.scalar_tensor_tensor(out=cv, in0=ub_[:, 2:UW], scalar=cw[:, 2 * NFC + fc:2 * NFC + fc + 1], in1=cv, op0=ALU.mult, op1=ALU.add),
                                reads=[("ubuf", ui), ("small",)] + cres, writes=cres)
                          tr.op("act", lambda e, cv=cv: e.activation(out=cv, in_=cv, func=AF.Gelu_apprx_tanh), reads=cres, writes=cres)
                      for ch in range(4):
                          bk = fm_project(wsg, lambda kc, wvg=wvg, ch=ch: wvg[:, kc, ch * 128:(ch + 1) * 128], KC, lambda kc: hT[:, kc, 0:L], HT_ALL, L)
                          fsegs = [(0, PT, 0)] + [(PT + 8 * s, PT + 8 * s + 8, PT + 2 + 10 * s) for s in range(NSQ)]
                          for (lo, hi, ci_) in fsegs:
                              tr.op("dve", lambda e, bk=bk, lo=lo, hi=hi, ci_=ci_, ch=ch, fi=fi: e.tensor_tensor(out=fT[:, fi, ch, lo:hi], in0=ps[bk][:, lo:hi], in1=cg[:, ch, ci_:ci_ + (hi - lo)], op=ALU.mult),
                                    reads=[("ps", bk), ("cg", ch)] + RTOK, writes=[("fT", fi)])
                      wsd, wvd = wload(d_wdn[l][512 * j:512 * (j + 1), :].rearrange("(c q) n -> q c n", q=128), [4, 2048])
                      for oc in range(KC):
                          bk = fm_project(wsd, lambda c, wvd=wvd, oc=oc: wvd[:, c, oc * 128:(oc + 1) * 128], 4, lambda c, fi=fi: fT[:, fi, c, :], [("fT", fi)], L)
                          tr.op("dve", lambda e, bk=bk, oc=oc, c0=c0: e.tensor_tensor(out=x[:, oc, c0:c0 + L], in0=x[:, oc, c0:c0 + L], in1=ps[bk][:, 0:L], op=ALU.add),
                                reads=[("ps", bk), ("x", oc)], writes=[("x", oc)])
                  store(o_convp[l, p], convo[:, 0], [("convo",)])
                  store(o_convs[l, p * NSQ + 0], convo[:, 1], [("convo",)])
                  release([("fT", 0), ("fT", 1)])
                  if cfg.stop == "F" and p == cfg.stop_pass:
                      raise _Stop()
                  if l == DEP - 1:
                      rmsnorm(p, gfin, final=True)
                  else:
                      for kc in range(KC):
                          store(d_xs[:, kc, p * L:(p + 1) * L], x[:, kc, :], [("x", kc)], writes=[("dram", "xs", p)])


        try:
            body()
        except _Stop:
            pass
        tr.finalize()
        tr.replay(nc, block, sems, dma_sems)
    return nc


_PROG_CACHE = {}


def _run(cfg, inp, trace=False):
    DEP, P, L, PT, NSQ = cfg.depth, cfg.npass, cfg.L, cfg.PT, cfg.nsq
    NS = NSQ * P
    f = lambda a: np.ascontiguousarray(np.asarray(a, dtype=np.float32))
    xp, xs = f(inp["x_prompt"]), f(inp["x_sample"])
    ck, cvv = f(inp["cache_win_k"]), f(inp["cache_win_v"])
    sr, sc = f(inp["state_ret"]), f(inp["state_conv"])
    W = {k: f(inp[k])[:DEP] for k in ("w_in", "w_proj_a", "w_proj_b", "w_o", "w_up", "w_down")}
    if cfg.stop is not None and cfg.stop[0] in "NWKGAB":
        for k in ("w_proj_a", "w_proj_b", "w_o", "w_up", "w_down"):
            W[k] = np.zeros((1, 128, 8), np.float32)
    small = np.zeros((128, DEP * SM + 16), np.float32)
    for l in range(DEP):
        b = l * SM
        small[:, b:b + 16] = f(inp["g_mix"])[l].reshape(16, 128).T
        small[:, b + 16:b + 32] = f(inp["g_ffn"])[l].reshape(16, 128).T
        small[:, b + 32:b + 32 + 3 * NFC] = f(inp["conv_w"])[l].reshape(3, NFC, 128).transpose(2, 0, 1).reshape(128, 3 * NFC)
        small[:, b + 32 + 3 * NFC:b + 32 + 4 * NFC] = f(inp["conv_b"])[l].reshape(NFC, 128).T
        small[:, b + 32 + 4 * NFC:b + 48 + 4 * NFC] = f(inp["sinks"])[l][None, :]
    small[:, DEP * SM:] = f(inp["g_final"]).reshape(16, 128).T
    in_maps = []
    for c in range(NCORES):
        seqi = c % 2
        xT = np.empty((128, KC, cfg.TOK), np.float32)
        kcT = np.empty((DEP, NS, 128, 2, 2, 128), np.float32)
        vc2 = np.empty((DEP, NS, 128, 4, 2, 64), np.float32)
        sret = np.empty((DEP, NS, 128, 8, 128), np.float32)
        sconv = np.empty((DEP, 128, NS, NFC, 2), np.float32)
        for p in range(P):
            t0 = PT * p
            xT[:, :, p * L:p * L + PT] = xp[seqi, t0:t0 + PT, :].T.reshape(KC, 128, PT).transpose(1, 0, 2)
            for s in range(NSQ):
                sl = p * NSQ + s
                gs = 4 * c + (sl % 4)
                xT[:, :, p * L + PT + 8 * s:p * L + PT + 8 * s + 8] = xs[gs].T.reshape(KC, 128, 8).transpose(1, 0, 2)
                for l in range(DEP):
                    ckT = ck[l, gs].transpose(1, 2, 0)
                    for v in range(2):
                        for cc in range(2):
                            kcT[l, sl, 0:64, v, cc, :] = ckT[2 * cc + v]
                            kcT[l, sl, 64:128, v, cc, :] = ckT[2 * cc + 1 - v]
                    vc2[l, sl] = np.repeat(cvv[l, gs][:, :, None, :], 2, axis=2)
                    sret[l, sl] = sr[l, gs].transpose(1, 0, 2)
                    sconv[l, :, sl] = sc[l, gs].reshape(2, NFC, 128).transpose(2, 1, 0)
        m = {"xT": xT, "tabs": build_tabs(cfg, c), "small": small,
             "kcT": kcT.reshape(DEP, NS, 128, 512), "vc2": vc2.reshape(DEP, NS, 128, 512),
             "sret": sret.reshape(DEP, NS, 128, 1024), "sconv": sconv}
        m.update(W)
        in_maps.append(m)
    key = (cfg.depth, cfg.nblk, cfg.npass)
    if key not in _PROG_CACHE:
        _PROG_CACHE[key] = build_program(cfg)
    nc = _PROG_CACHE[key]
    res = run_bass_kernel_spmd(nc, in_maps, core_ids=list(range(NCORES)), **({"trace": True} if trace else {}))
    R = res.results
    SEQ = cfg.seq
    y_p = np.empty((2, SEQ, D), np.float32)
    NSG = 32
    y_s = np.empty((NSG, 8, D), np.float32)
    wkp = np.empty((DEP, 2, 128, 4, 64), np.float32); wvp = np.empty_like(wkp)
    retp = np.empty((DEP, 2, 8, 128, 128), np.float32)
    convp = np.empty((DEP, 2, 2, DFF), np.float32)
    wks = np.empty((DEP, NSG, 8, 4, 64), np.float32); wvs = np.empty_like(wks)
    rets = np.empty((DEP, NSG, 8, 128, 128), np.float32)
    convs = np.empty((DEP, NSG, 2, DFF), np.float32)
    for c in range(NCORES):
        seqi = c % 2
        o = R[c]
        yT = np.asarray(o["yT"])
        for p in range(P):
            t0 = PT * p
            if c < 2:
                y_p[seqi, t0:t0 + PT] = yT[:, :, p * L:p * L + PT].transpose(2, 1, 0).reshape(PT, D)
            for s in range(NSQ):
                sl = p * NSQ + s
                if sl >= 4:
                    continue
                gs = 4 * c + sl
                y_s[gs] = yT[:, :, p * L + PT + 8 * s:p * L + PT + 8 * s + 8].transpose(2, 1, 0).reshape(8, D)
                wks[:, gs] = np.asarray(o["wk_s"])[:, sl].reshape(DEP, 8, 4, 64)
                wvs[:, gs] = np.asarray(o["wv_s"])[:, sl].reshape(DEP, 8, 4, 64)
                rets[:, gs] = np.asarray(o["ret_s"])[:, sl].reshape(DEP, 128, 8, 128).transpose(0, 2, 1, 3)
                convs[:, gs] = np.asarray(o["conv_s"])[:, sl].transpose(0, 3, 2, 1).reshape(DEP, 2, DFF)
        if c < 2:
            wkp[:, seqi] = np.asarray(o["wk_p"])[:, P - 1].reshape(DEP, 128, 4, 64)
            wvp[:, seqi] = np.asarray(o["wv_p"])[:, P - 1].reshape(DEP, 128, 4, 64)
            retp[:, seqi] = np.asarray(o["ret_p"])[:, P - 1].reshape(DEP, 128, 8, 128).transpose(0, 2, 1, 3)
            convp[:, seqi] = np.asarray(o["conv_p"])[:, P - 1].transpose(0, 3, 2, 1).reshape(DEP, 2, DFF)
    outs = (y_p, y_s, wkp, wvp, retp, convp, wks, wvs, rets, convs)
    if trace:
        return outs, res
    return outs


def kernel(**inputs):
    return _run(Cfg(), inputs)
```
